# Optimizing a Trainium2 kernel written in Bass

```python
import math
import jax
import jax.numpy as jnp
from jax import lax
import numpy as np

D_MODEL = 1024
BATCH = 8
SEQ = 4096
DEPTH = 2

CHUNK = 64
Q_BLOCK = 128
LN_EPS = 1e-5
DEEPNORM_ALPHA = (2 * DEPTH) ** 0.25
DEEPNORM_BETA = (8 * DEPTH) ** -0.25

D_FF = 2816
FFN_RES = 0.5

H_A = 8
HD_A = 64
ROT_A = HD_A // 4
ROPE_THETA = 500000.0
A_W = H_A * 2 * HD_A

H_B = 16
P_B = 64
D_INNER = H_B * P_B
G_B = 2
N_B = 128
CONV_B = 4
CONV_DIM = D_INNER + 2 * G_B * N_B

H_C = 16
HD_C = 64
C_W = H_C * HD_C

N_BRANCH = 3
BR_W = 1024

_SEGMENTS = (A_W, A_W, A_W, D_INNER, CONV_DIM, H_B, C_W, C_W, C_W, N_BRANCH * D_MODEL)
SPLIT_POINTS = tuple(int(v) for v in np.cumsum(_SEGMENTS)[:-1])
N_IN = int(sum(_SEGMENTS))

kernel_name = 'hybrid_diffattn_ssd_stickbreak_macaron'


def _layernorm(x, g, b):
    xf = x.astype(jnp.float32)
    mu = jnp.mean(xf, -1, keepdims=True)
    xc = xf - mu
    var = jnp.mean(xc * xc, -1, keepdims=True)
    y = xc * lax.rsqrt(var + LN_EPS) * g.astype(jnp.float32) + b.astype(jnp.float32)
    return y.astype(x.dtype)


def _rmsnorm(x, g):
    xf = x.astype(jnp.float32)
    y = xf * lax.rsqrt(jnp.mean(xf * xf, -1, keepdims=True) + LN_EPS)
    return (y * g.astype(jnp.float32)).astype(x.dtype)


def _swiglu(x, w_in, w_out):
    gate, up = jnp.split(x @ w_in, 2, axis=-1)
    return (jax.nn.silu(gate) * up) @ w_out


def _rotary_tables(s):
    half = ROT_A // 2
    inv_freq = ROPE_THETA ** (-jnp.arange(half, dtype=jnp.float32) * 2.0 / ROT_A)
    ang = jnp.arange(s, dtype=jnp.float32)[:, None] * inv_freq[None, :]
    return jnp.cos(ang), jnp.sin(ang)


def _apply_partial_rope(x, cos, sin):
    half = ROT_A // 2
    c = cos.reshape(cos.shape[0], 1, 1, half).astype(x.dtype)
    sn = sin.reshape(sin.shape[0], 1, 1, half).astype(x.dtype)
    x1 = x[..., :half]
    x2 = x[..., half:ROT_A]
    return jnp.concatenate([x1 * c - x2 * sn, x2 * c + x1 * sn, x[..., ROT_A:]], axis=-1)


def _diff_attention(qa, ka, va, lam, subln_g, layer):
    b, s, _ = qa.shape
    nb = s // Q_BLOCK
    cos, sin = _rotary_tables(s)
    q = _apply_partial_rope(qa.reshape(b, s, H_A, 2, HD_A), cos, sin)
    k = _apply_partial_rope(ka.reshape(b, s, H_A, 2, HD_A), cos, sin)
    v = va.reshape(b, s, H_A, 2 * HD_A)
    lam_init = 0.8 - 0.6 * math.exp(-0.3 * layer)
    lf = lam.astype(jnp.float32)
    lam_full = jnp.exp(jnp.sum(lf[0] * lf[1])) - jnp.exp(jnp.sum(lf[2] * lf[3])) + lam_init
    kpos = jnp.arange(s)
    kchunk = kpos // CHUNK
    qpos = kpos.reshape(nb, Q_BLOCK)
    qb = jnp.moveaxis(q.reshape(b, nb, Q_BLOCK, H_A, 2, HD_A), 1, 0)
    scale = HD_A ** -0.5

    def block(args):
        qblk, qp = args
        sc = jnp.einsum('bqhmd,bkhmd->bhmqk', qblk, k).astype(jnp.float32) * scale
        allowed = kchunk[None, :] <= (qp // CHUNK)[:, None]
        p = jax.nn.softmax(jnp.where(allowed, sc, -jnp.inf), axis=-1)
        w = p[:, :, 0] - lam_full * p[:, :, 1]
        return jnp.einsum('bhqk,bkhe->bqhe', w.astype(v.dtype), v)

    out = lax.map(block, (qb, qpos))
    out = jnp.moveaxis(out, 0, 1).reshape(b, s, H_A, 2 * HD_A)
    out = _rmsnorm(out, subln_g) * (1.0 - lam_init)
    return out.reshape(b, s, A_W)


def _stick_breaking(qc, kc, vc):
    b, s, _ = qc.shape
    nb = s // Q_BLOCK
    k = kc.reshape(b, s, H_C, HD_C)
    v = vc.reshape(b, s, H_C, HD_C)
    qb = jnp.moveaxis(qc.reshape(b, nb, Q_BLOCK, H_C, HD_C), 1, 0)
    kpos = jnp.arange(s)
    qpos = kpos.reshape(nb, Q_BLOCK)
    scale = HD_C ** -0.5

    def block(args):
        qblk, qp = args
        z = jnp.einsum('bqhd,bkhd->bhqk', qblk, k).astype(jnp.float32) * scale
        strict = kpos[None, :] < qp[:, None]
        log_stay = jnp.where(strict, jax.nn.log_sigmoid(-z), 0.0)
        after = lax.cumsum(log_stay, axis=3, reverse=True) - log_stay
        a = jnp.where(strict, jnp.exp(jax.nn.log_sigmoid(z) + after), 0.0)
        return jnp.einsum('bhqk,bkhd->bqhd', a.astype(v.dtype), v)

    out = lax.map(block, (qb, qpos))
    return jnp.moveaxis(out, 0, 1).reshape(b, s, C_W)


def _ssd(x, dt, a_neg, bm, cm):
    b, s = x.shape[:2]
    nc = s // CHUNK
    hg = H_B // G_B
    xr = x.reshape(b, nc, CHUNK, G_B, hg, P_B)
    dtr = dt.reshape(b, nc, CHUNK, G_B, hg)
    br = bm.reshape(b, nc, CHUNK, G_B, N_B)
    cr = cm.reshape(b, nc, CHUNK, G_B, N_B)
    acum = jnp.cumsum(dtr * a_neg.reshape(G_B, hg), axis=2)
    xdt = xr * dtr[..., None]
    causal = jnp.tril(jnp.ones((CHUNK, CHUNK), dtype=bool))[None, None, :, :, None, None]
    seg = acum[:, :, :, None] - acum[:, :, None, :]
    decay = jnp.exp(jnp.where(causal, seg, -jnp.inf))
    cb = jnp.einsum('bctgn,bcsgn->bctsg', cr, br)
    y_diag = jnp.einsum('bctsgh,bcsghp->bctghp', cb[..., None] * decay, xdt)
    decay_to_end = jnp.exp(acum[:, :, -1:] - acum)
    states = jnp.einsum('bclgn,bclghp->bcghpn', br, xdt * decay_to_end[..., None])
    chunk_decay = jnp.exp(acum[:, :, -1])

    def step(h_prev, inp):
        st, dc = inp
        return h_prev * dc[..., None, None] + st, h_prev

    h0 = jnp.zeros((b, G_B, hg, P_B, N_B), jnp.float32)
    _, h_in = lax.scan(step, h0, (jnp.moveaxis(states, 1, 0), jnp.moveaxis(chunk_decay, 1, 0)))
    h_in = jnp.moveaxis(h_in, 0, 1)
    y_off = jnp.einsum('bclgn,bcghpn->bclghp', cr, h_in) * jnp.exp(acum)[..., None]
    return (y_diag + y_off).reshape(b, s, H_B, P_B)


def _mamba2(zb, xbc, dtb, conv_w, conv_b, dt_bias, a_log, d_skip, norm_g):
    b, s, _ = zb.shape
    xbc = lax.conv_general_dilated(
        xbc, conv_w[:, None, :], window_strides=(1,), padding=[(CONV_B - 1, 0)],
        dimension_numbers=('NWC', 'WIO', 'NWC'), feature_group_count=CONV_DIM) + conv_b
    xbc = jax.nn.silu(xbc).astype(jnp.float32)
    xs, bm, cm = jnp.split(xbc, [D_INNER, D_INNER + G_B * N_B], axis=-1)
    dt = jax.nn.softplus(dtb.astype(jnp.float32) + dt_bias.astype(jnp.float32))
    a_neg = -jnp.exp(a_log.astype(jnp.float32))
    xh = xs.reshape(b, s, H_B, P_B)
    y = _ssd(xh, dt, a_neg, bm.reshape(b, s, G_B, N_B), cm.reshape(b, s, G_B, N_B))
    y = y + d_skip.astype(jnp.float32)[:, None] * xh
    y = y.reshape(b, s, D_INNER) * jax.nn.silu(zb.astype(jnp.float32))
    y = _rmsnorm(y.reshape(b, s, G_B, D_INNER // G_B), norm_g.reshape(G_B, D_INNER // G_B))
    return y.reshape(b, s, D_INNER).astype(zb.dtype)


def _hybrid_mixer(h, layer, w_in, gate_bias, diff_lambda, diff_subln_g, conv_w, conv_b,
                  dt_bias, a_log, d_skip, norm_g, w_branch, w_out):
    b, s, _ = h.shape
    qa, ka, va, zb, xbc, dtb, qc, kc, vc, gpre = jnp.split(h @ w_in, SPLIT_POINTS, axis=-1)
    o_a = _diff_attention(qa, ka, va, diff_lambda, diff_subln_g, layer)
    o_b = _mamba2(zb, xbc, dtb, conv_w, conv_b, dt_bias, a_log, d_skip, norm_g)
    o_c = _stick_breaking(qc, kc, vc)
    branches = jnp.stack([o_a, o_b.astype(h.dtype), o_c], axis=2)
    proj = jnp.einsum('bsrc,rcd->bsrd', branches, w_branch)
    gates = jax.nn.sigmoid(gpre.reshape(b, s, N_BRANCH, D_MODEL) + gate_bias)
    merged = jnp.einsum('bsrd,bsrd->bsd', gates, proj)
    return merged @ w_out


def setup_inputs(seed: int = 0) -> dict:
    key = jax.random.key(seed)
    ks = jax.random.split(key, 24)
    f32 = jnp.float32
    L = DEPTH

    def nrm(k, shape, scale):
        return jax.random.normal(k, shape, f32) * scale

    dt0 = jnp.exp(jax.random.uniform(ks[11], (L, H_B), f32, math.log(1e-3), math.log(1e-1)))
    return {
        'x': nrm(ks[0], (BATCH, SEQ, D_MODEL), 1.0),
        'ffn1_w_in': nrm(ks[1], (L, D_MODEL, 2 * D_FF), DEEPNORM_BETA * D_MODEL ** -0.5),
        'ffn1_w_out': nrm(ks[2], (L, D_FF, D_MODEL), DEEPNORM_BETA * D_FF ** -0.5),
        'ln1_g': 1.0 + nrm(ks[3], (L, D_MODEL), 0.02),
        'ln1_b': nrm(ks[4], (L, D_MODEL), 0.02),
        'w_mix_in': nrm(ks[5], (L, D_MODEL, N_IN), D_MODEL ** -0.5),
        'gate_bias': nrm(ks[6], (L, N_BRANCH, D_MODEL), 0.02),
        'diff_lambda': nrm(ks[7], (L, 4, HD_A), 0.1),
        'diff_subln_g': 1.0 + nrm(ks[8], (L, 2 * HD_A), 0.02),
        'ssm_conv_w': nrm(ks[9], (L, CONV_B, CONV_DIM), CONV_B ** -0.5),
        'ssm_conv_b': nrm(ks[10], (L, CONV_DIM), 0.02),
        'ssm_dt_bias': dt0 + jnp.log(-jnp.expm1(-dt0)),
        'ssm_A_log': jnp.log(jax.random.uniform(ks[12], (L, H_B), f32, 1.0, 16.0)),
        'ssm_D': 1.0 + nrm(ks[13], (L, H_B), 0.02),
        'ssm_norm_g': 1.0 + nrm(ks[14], (L, D_INNER), 0.02),
        'w_branch': nrm(ks[15], (L, N_BRANCH, BR_W, D_MODEL), DEEPNORM_BETA * BR_W ** -0.5),
        'w_mix_out': nrm(ks[16], (L, D_MODEL, D_MODEL), DEEPNORM_BETA * D_MODEL ** -0.5),
        'ln2_g': 1.0 + nrm(ks[17], (L, D_MODEL), 0.02),
        'ln2_b': nrm(ks[18], (L, D_MODEL), 0.02),
        'ffn2_w_in': nrm(ks[19], (L, D_MODEL, 2 * D_FF), DEEPNORM_BETA * D_MODEL ** -0.5),
        'ffn2_w_out': nrm(ks[20], (L, D_FF, D_MODEL), DEEPNORM_BETA * D_FF ** -0.5),
        'ln3_g': 1.0 + nrm(ks[21], (L, D_MODEL), 0.02),
        'ln3_b': nrm(ks[22], (L, D_MODEL), 0.02),
    }


def reference(x, ffn1_w_in, ffn1_w_out, ln1_g, ln1_b, w_mix_in, gate_bias, diff_lambda,
              diff_subln_g, ssm_conv_w, ssm_conv_b, ssm_dt_bias, ssm_A_log, ssm_D, ssm_norm_g,
              w_branch, w_mix_out, ln2_g, ln2_b, ffn2_w_in, ffn2_w_out, ln3_g, ln3_b):
    for l in range(DEPTH):
        x = _layernorm(DEEPNORM_ALPHA * x + FFN_RES * _swiglu(x, ffn1_w_in[l], ffn1_w_out[l]),
                       ln1_g[l], ln1_b[l])
        mix = _hybrid_mixer(x, l, w_mix_in[l], gate_bias[l], diff_lambda[l], diff_subln_g[l],
                            ssm_conv_w[l], ssm_conv_b[l], ssm_dt_bias[l], ssm_A_log[l], ssm_D[l],
                            ssm_norm_g[l], w_branch[l], w_mix_out[l])
        x = _layernorm(DEEPNORM_ALPHA * x + mix, ln2_g[l], ln2_b[l])
        x = _layernorm(DEEPNORM_ALPHA * x + FFN_RES * _swiglu(x, ffn2_w_in[l], ffn2_w_out[l]),
                       ln3_g[l], ln3_b[l])
    return x
```

```python
import math
from contextlib import ExitStack
import numpy as np
import concourse.bass as bass
import concourse.mybir as mybir
from concourse.bass_utils import run_bass_kernel_spmd

F32 = mybir.dt.float32
BF16 = mybir.dt.bfloat16
AF = mybir.ActivationFunctionType
ALU = mybir.AluOpType

D = 1024
DFF = 2816
DEPTH = 2
ALPHA = (2 * DEPTH) ** 0.25
LN_EPS = 1e-5
NIN = 11792


import os
SKIP_SAME_ENGINE = True
_UID = [0]


def uname(n):
    _UID[0] += 1
    return "%s_u%d" % (n, _UID[0])


class Tok:
    __slots__ = ("sem", "val", "key")

    def __init__(self, sem, val, key):
        self.sem, self.val, self.key = sem, val, key


class DSem:
    def __init__(self, sem, key):
        self.sem, self.key, self.cnt = sem, key, 0


class K:
    def __init__(self, nc, es):
        self.nc, self.es = nc, es
        self.engs = {}
        for name, e in (("pe", nc.tensor), ("act", nc.scalar), ("dve", nc.vector),
                        ("pool", nc.gpsimd), ("sp", nc.sync)):
            sem = es.enter_context(nc.semaphore("s_" + name))
            self.engs[name] = {"e": e, "sem": sem, "cnt": 0, "waited": {}, "name": name}
        self.dsems = []
        self.uid = 0

    def begin_phase(self):
        self.pool_i = 0

    def dsem(self, name):
        i = getattr(self, "pool_i", 0)
        self.pool_i = i + 1
        if i < len(self.dsems):
            return self.dsems[i]
        sem = self.es.enter_context(self.nc.semaphore("d_pool_%d" % i))
        d = DSem(sem, "d_pool_%d" % i)
        self.dsems.append(d)
        return d

    def wait(self, engname, deps):
        E = self.engs[engname]
        for t in deps:
            if t is None:
                continue
            if isinstance(t, (list, tuple)):
                self.wait(engname, t)
                continue
            if SKIP_SAME_ENGINE and t.key == "e_pe" and engname == "pe":
                continue
            if E["waited"].get(t.key, 0) < t.val:
                E["e"].wait_ge(t.sem, t.val)
                E["waited"][t.key] = t.val

    def op(self, engname, fn, deps=(), sig=True):
        E = self.engs[engname]
        self.wait(engname, deps)
        inst = fn(E["e"])
        if sig:
            E["cnt"] += 1
            inst.then_inc(E["sem"], 1)
            return Tok(E["sem"], E["cnt"], "e_" + engname)
        return None

    def last(self, engname):
        E = self.engs[engname]
        if E["cnt"] == 0:
            return None
        return Tok(E["sem"], E["cnt"], "e_" + engname)

    def dma(self, q, out, in_, dsem, deps=()):
        E = self.engs[q]
        self.wait(q, deps)
        inst = E["e"].dma_start(out=out, in_=in_)
        dsem.cnt += 16
        inst.then_inc(dsem.sem, 16)
        return Tok(dsem.sem, dsem.cnt, dsem.key)

    def barrier(self):
        toks = [self.last(n) for n in ("pe", "act", "dve", "pool")]
        toks += [Tok(d.sem, d.cnt, d.key) for d in self.dsems if d.cnt > 0]
        for n in ("pe", "act", "dve", "pool", "sp"):
            self.wait(n, toks)


def _mm(out, lhsT, rhs, start, stop):
    return lambda e: e.matmul(out, lhsT, rhs, start=start, stop=stop)


def load_cast_weights(k, es, w_dram, w_sb, nrow_chunks, ncols, colblk, stg_n=4, tag="w"):
    nc = k.nc
    stg = [es.enter_context(nc.sbuf_tensor(uname("stg_%s_%d" % (tag, i)), [128, colblk], F32)) for i in range(stg_n)]
    dss = [k.dsem("stg") for _ in range(stg_n)]
    free = [None] * stg_n
    engs = ("dve", "act", "dve")
    i = 0
    last = []
    for kc in range(nrow_chunks):
        for c0 in range(0, ncols, colblk):
            cw = min(colblk, ncols - c0)
            s = i % stg_n
            t = k.dma("sp", stg[s][:, 0:cw], w_dram[kc * 128:(kc + 1) * 128, c0:c0 + cw], dss[s], deps=[free[s]])
            en = engs[i % 3]
            if en == "act":
                free[s] = k.op(en, lambda e, s=s, kc=kc, c0=c0, cw=cw: e.copy(out=w_sb[:, kc, c0:c0 + cw], in_=stg[s][:, 0:cw]), deps=[t])
            else:
                free[s] = k.op(en, lambda e, s=s, kc=kc, c0=c0, cw=cw: e.tensor_copy(out=w_sb[:, kc, c0:c0 + cw], in_=stg[s][:, 0:cw]), deps=[t])
            i += 1
    return [f for f in free if f is not None]


def emit_ffn(k, x_in, x_out, w_in, w_out, g, b, ident, S):
    nc = k.nc
    k.begin_phase()
    TG = 256
    NG = S // TG
    NJ = DFF // 128
    with ExitStack() as es:
        sb = lambda name, shape, dt: es.enter_context(nc.sbuf_tensor(uname(name), shape, dt))
        w1 = sb("w1", [128, 8, 2 * DFF], BF16)
        w2 = sb("w2", [128, NJ, D], BF16)
        gb = sb("gb", [128, D], F32)
        bb = sb("bb", [128, D], F32)
        idb = sb("idb", [128, 128], BF16)
        xin = [sb("xin%d" % i, [128, 2, D], F32) for i in range(2)]
        xb = sb("xb", [128, 2, D], BF16)
        xT = [sb("xT%d" % i, [128, 8, TG], BF16) for i in range(2)]
        hT = sb("hT", [128, NJ, TG], BF16)
        sg = [sb("sg%d" % i, [128, TG], F32) for i in range(2)]
        st6 = sb("st6", [128, 2, 6], F32)
        mv = sb("mv", [128, 2], F32)
        sd = sb("sd", [128, 1], F32)
        rs = sb("rs", [128, 1], F32)
        nm = sb("nm", [128, 1], F32)
        ps_tp = es.enter_context(nc.psum_tensor(uname("ps_tp"), [128, 2, D], BF16))
        ps_gu = es.enter_context(nc.psum_tensor(uname("ps_gu"), [128, 2, 2, TG], F32))
        ps_y = es.enter_context(nc.psum_tensor(uname("ps_y"), [128, 2, D], F32))
        d_c = k.dsem("const")
        d_x = [k.dsem("xin") for _ in range(2)]
        d_o = [k.dsem("xout") for _ in range(2)]

        tc = [k.dma("sp", gb[:], g.partition_broadcast(128), d_c),
              k.dma("sp", bb[:], b.partition_broadcast(128), d_c),
              k.dma("sp", idb[:], ident, d_c)]
        tc = [tc[-1]]
        wt = load_cast_weights(k, es, w_in, w1, 8, 2 * DFF, 704, tag="w1")
        wt += load_cast_weights(k, es, w_out, w2, NJ, D, 512, tag="w2")

        xv_in = x_in.rearrange("(g t p) d -> g p t d", p=128, t=2)
        xv_out = x_out.rearrange("(g t p) d -> g p t d", p=128, t=2)

        x_tok = [None, None]
        store_tok = [None, None]
        xT_free = [None, None]
        hT_free = None
        gu_free = [None, None]
        sg_free = [None, None]
        y_free = [None, None]
        xb_free = None
        tp_free = [None, None]

        def load(gi):
            s = gi % 2
            x_tok[s] = k.dma("sp", xin[s][:], xv_in[gi], d_x[s], deps=[store_tok[s]])

        load(0)
        for gi in range(NG):
            s = gi % 2
            if gi + 1 < NG:
                load(gi + 1)
            t_xb = k.op("act", lambda e: e.copy(out=xb[:], in_=xin[s][:]), deps=[x_tok[s], xb_free])
            ev = []
            for t in range(2):
                for kc in range(8):
                    tk = k.op("pe", lambda e, t=t, kc=kc: e.transpose(ps_tp[:, t, kc * 128:(kc + 1) * 128], xb[:, t, kc * 128:(kc + 1) * 128], idb[:]),
                              deps=[t_xb, tc, tp_free[t]], sig=(kc == 7))
                tp_free[t] = k.op("dve", lambda e, t=t: e.tensor_copy(out=xT[s][:, :, t * 128:(t + 1) * 128],
                                                                     in_=ps_tp[:, t, :].rearrange("p (c q) -> p c q", q=128)),
                                  deps=[tk, xT_free[s]])
                ev.append(tp_free[t])
            xb_free = tk
            for j in range(NJ):
                bf = j % 2
                for half in range(2):
                    c0 = half * DFF + j * 128
                    for kc in range(8):
                        tk = k.op("pe", _mm(ps_gu[:, bf, half, :], w1[:, kc, c0:c0 + 128], xT[s][:, kc, :], kc == 0, kc == 7),
                                  deps=[ev, wt, gu_free[bf]], sig=(kc == 7 and half == 1))
                t_a = k.op("act", lambda e, bf=bf: e.activation(out=sg[bf][:], in_=ps_gu[:, bf, 0, :], func=AF.Silu),
                           deps=[tk, sg_free[bf]])
                t_d = k.op("dve", lambda e, bf=bf, j=j: e.tensor_tensor(out=hT[:, j, :], in0=sg[bf][:], in1=ps_gu[:, bf, 1, :], op=ALU.mult),
                           deps=[t_a, tk, hT_free])
                gu_free[bf] = t_d
                sg_free[bf] = t_d
            xT_free[s] = tk
            t_h = t_d
            for t in range(2):
                for n in range(2):
                    for j in range(NJ):
                        tk = k.op("pe", _mm(ps_y[:, t, n * 512:(n + 1) * 512], hT[:, j, t * 128:(t + 1) * 128], w2[:, j, n * 512:(n + 1) * 512], j == 0, j == NJ - 1),
                                  deps=[t_h, y_free[t]], sig=(j == NJ - 1 and n == 1))
                xt = xin[s][:, t, :]
                t1 = k.op("act", lambda e, xt=xt: e.mul(out=xt, in_=xt, mul=ALPHA), deps=[xb_free, t_xb])
                t2 = k.op("dve", lambda e, xt=xt, t=t: e.scalar_tensor_tensor(out=xt, in0=ps_y[:, t, :], scalar=0.5, in1=xt, op0=ALU.mult, op1=ALU.add),
                          deps=[t1, tk])
                y_free[t] = t2
                tl = emit_ln_tail(k, xt, gb, bb, st6, mv, sd, rs, nm, t2)
                store_tok[s] = k.dma("sp", xv_out[gi][:, t, :], xt, d_o[s], deps=[tl])
            hT_free = tk
        k.barrier()


def emit_ln_tail(k, xt, gb, bb, st6, mv, sd, rs, nm, dep):
    ta = None
    for h in range(2):
        ta = k.op("dve", lambda e, h=h: e.bn_stats(out=st6[:, h, :], in_=xt[:, h * 512:(h + 1) * 512]), deps=[dep, ta])
    t3 = k.op("dve", lambda e: e.bn_aggr(out=mv[:], in_=st6[:].rearrange("p a b -> p (a b)")), deps=[ta])
    t4 = k.op("act", lambda e: e.activation(out=sd[:], in_=mv[:, 1:2], func=AF.Sqrt, bias=LN_EPS_AP[0][:, 0:1], scale=1.0), deps=[t3, LN_EPS_AP[1]])
    t5 = k.op("dve", lambda e: e.reciprocal(out=rs[:], in_=sd[:]), deps=[t4])
    t6 = k.op("dve", lambda e: e.tensor_scalar(out=nm[:], in0=mv[:, 0:1], scalar1=rs[:, 0:1], scalar2=-1.0, op0=ALU.mult, op1=ALU.mult), deps=[t5])
    t7 = k.op("act", lambda e: e.activation(out=xt, in_=xt, func=AF.Identity, bias=nm[:, 0:1], scale=rs[:, 0:1]), deps=[t6, t5])
    t8 = k.op("pool", lambda e: e.tensor_tensor(out=xt, in0=xt, in1=gb[:], op=ALU.mult), deps=[t7])
    t9 = k.op("pool", lambda e: e.tensor_tensor(out=xt, in0=xt, in1=bb[:], op=ALU.add), deps=[t8])
    return t9


LN_EPS_AP = [None, None]


def setup_consts(k, es):
    nc = k.nc
    eps = es.enter_context(nc.sbuf_tensor("c_eps", [128, 1], F32))
    LN_EPS_AP[1] = k.op("pool", lambda e: e.memset(eps[:], LN_EPS))
    LN_EPS_AP[0] = eps
    one = es.enter_context(nc.sbuf_tensor("c_one", [128, 1], F32))
    ONE_AP[1] = k.op("pool", lambda e: e.memset(one[:], 1.0))
    ONE_AP[0] = one


class TB:
    __slots__ = ("ap", "w", "r")

    def __init__(self, ap):
        self.ap, self.w, self.r = ap, None, {}

    def __getitem__(self, idx):
        return self.ap[idx]


def _deps_of(outs, ins, extra):
    deps = list(extra)
    for o in outs:
        deps += list(o.r.values())
        deps.append(o.w)
    for i in ins:
        deps.append(i.w)
    return deps


def _mark(tok, outs, ins):
    for o in outs:
        o.w = tok
        o.r = {}
    for i in ins:
        if i in outs:
            continue
        old = i.r.get(tok.key)
        if old is None or old.val < tok.val:
            i.r[tok.key] = tok


def OP(k, eng, fn, outs, ins, extra=()):
    tok = k.op(eng, fn, _deps_of(outs, ins, extra))
    _mark(tok, outs, ins)
    return tok


def MMG(k, out, mms, ins, extra=(), first=True, last=True, start=None):
    deps = _deps_of([out] if first else [], ins, extra)
    if not first:
        deps.append(out.w)
    n = len(mms)
    tok = None
    for i, (o, l, r) in enumerate(mms):
        st_ = (first if start is None else start) and i == 0
        tok = k.op("pe", _mm(o, l, r, st_, last and i == n - 1), deps if i == 0 else (), sig=(i == n - 1))
    if first:
        _mark(tok, [out], ins)
    else:
        out.w = tok
        _mark(tok, [], ins)
    return tok


def DMA(k, q, out_ap, in_ap, dsem, outs, ins, extra=()):
    tok = k.dma(q, out_ap, in_ap, dsem, _deps_of(outs, ins, extra))
    _mark(tok, outs, ins)
    return tok


def run_pipeline(jobs, stages, offs=None):
    n, L = len(jobs), len(stages)
    offs = offs or list(range(L))
    for t in range(n + max(offs)):
        for si, st in enumerate(stages):
            j = t - offs[si]
            if 0 <= j < n:
                st(jobs[j])


def DMAG(k, q, pairs, dsem, outs, ins, extra=()):
    deps = _deps_of(outs, ins, extra)
    tok = None
    for i, (o, a) in enumerate(pairs):
        tok = k.dma(q, o, a, dsem, deps if i == 0 else ())
    _mark(tok, outs, ins)
    return tok


SCRATCH_KIND = "ExternalOutput"
OFF_QA, OFF_KA, OFF_VA, OFF_ZB, OFF_XBC, OFF_DT, OFF_QC, OFF_KC, OFF_VC, OFF_G = 0, 1024, 2048, 3072, 4096, 5632, 5648, 6672, 7696, 8720


def emit_diffattn(k, hT_d, oaT_d, w_mix, lam_d, subg_d, cos_d, sin_d, cst, S, layer, stage=9, hT_res=None):
    nc = k.nc
    k.begin_phase()
    NQG = S // 512
    NTB = S // 128
    lam_init = 0.8 - 0.6 * math.exp(-0.3 * layer)
    with ExitStack() as es:
        sb = lambda name, shape, dt: TB(es.enter_context(nc.sbuf_tensor(uname(name), shape, dt)))
        hT = hT_res if hT_res is not None else sb("hT", [128, 8, S], BF16)
        qT = sb("qT", [128, S], BF16)
        kz = [sb("kz%d" % i, [128, S], BF16) for i in range(2)]
        V = sb("V", [128, NTB, 128], BF16)
        cosT = sb("cosT", [128, S], F32)
        sinT = sb("sinT", [128, S], F32)
        wst = sb("wst", [128, 8, 384], F32)
        wA = sb("wA", [128, 8, 384], BF16)
        wrot = sb("wrot", [128, 8, 256], BF16)
        pT = [sb("pT%d" % i, [128, 512], BF16) for i in range(4)]
        t1 = sb("t1", [128, 512], F32)
        t2 = sb("t2", [128, 512], F32)
        e0 = sb("e0", [128, 512], F32)
        e1 = sb("e1", [128, 512], F32)
        e2 = sb("e2", [128, 512], F32)
        sq = sb("sq", [128, 512], BF16)
        oo = [sb("oo%d" % i, [128, 512], BF16) for i in range(2)]
        lamb = sb("lamb", [128, 4, 64], F32)
        lprod = sb("lprod", [128, 2, 64], F32)
        lsum = sb("lsum", [128, 2], F32)
        neglam = sb("neglam", [128, 1], F32)
        gsub = sb("gsub", [128, 1], F32)
        ones = sb("ones", [128, 128], BF16)
        mskA = sb("mskA", [128, 128], BF16)
        psum = es.enter_context(nc.psum_tensor(uname("ps"), [128, 8, 512], F32))
        ps = [TB(psum[:, i, :]) for i in range(8)]
        d_h = k.dsem("hT")
        d_c = k.dsem("c")
        d_w = k.dsem("w")
        d_o = [k.dsem("o") for _ in range(2)]

        hv = hT_d.rearrange("(c p) s -> p c s", p=128)
        if hT_res is None or hT_res.w is None:
            DMAG(k, "sp", [(hT.ap[:, kc, :], hv[:, kc, :]) for kc in range(8)], d_h, [hT], [])
        cp = [(cosT.ap[:], cos_d), (sinT.ap[:], sin_d), (mskA.ap[:], cst["maskA"])]
        cp.append((lamb.ap[:].rearrange("p a d -> p (a d)"), lam_d.partition_broadcast(128)))
        cp.append((gsub.ap[:], subg_d))
        DMAG(k, "sp", cp, d_c, [cosT, sinT, lamb, gsub, mskA], [])
        OP(k, "pool", lambda e: e.memset(ones.ap[:], 1.0), [ones], [])
        OP(k, "pool", lambda e: e.memset(wrot.ap[:], 0.0), [wrot], [])
        OP(k, "pool", lambda e: e.memset(kz[0].ap[:], 0.0), [kz[0]], [])
        OP(k, "pool", lambda e: e.memset(kz[1].ap[:], 0.0), [kz[1]], [])
        lv = lamb.ap[:].rearrange("p (a two) d -> p a two d", two=2)
        OP(k, "dve", lambda e: e.tensor_tensor(out=lprod.ap[:], in0=lv[:, :, 0, :], in1=lv[:, :, 1, :], op=ALU.mult), [lprod], [lamb])
        OP(k, "dve", lambda e: e.tensor_reduce(out=lsum.ap[:], in_=lprod.ap[:], axis=mybir.AxisListType.X, op=ALU.add), [lsum], [lprod])
        OP(k, "act", lambda e: e.activation(out=lsum.ap[:], in_=lsum.ap[:], func=AF.Exp), [lsum], [lsum])
        OP(k, "dve", lambda e: e.tensor_tensor(out=neglam.ap[:], in0=lsum.ap[:, 1:2], in1=lsum.ap[:, 0:1], op=ALU.subtract), [neglam], [lsum])
        OP(k, "dve", lambda e: e.tensor_scalar(out=neglam.ap[:], in0=neglam.ap[:], scalar1=-lam_init, scalar2=None, op0=ALU.add), [neglam], [neglam])
        OP(k, "dve", lambda e: e.tensor_scalar(out=gsub.ap[:], in0=gsub.ap[:], scalar1=1.0 - lam_init, scalar2=None, op0=ALU.mult), [gsub], [gsub])

        pi = 0
        def load_w_A(hA):
            DMAG(k, "sp", [(wst.ap[:, :, j * 128:(j + 1) * 128], w_mix[:, off + hA * 128: off + (hA + 1) * 128].rearrange("(c p) n -> p c n", p=128))
                           for j, off in enumerate((OFF_QA, OFF_KA, OFF_VA))], d_w, [wst], [])
            OP(k, "pool", lambda e: e.tensor_copy(out=wA.ap[:], in_=wst.ap[:]), [wA], [wst])
            for j in range(2):
                for m in range(2):
                    c = j * 128 + m * 64
                    OP(k, "dve", lambda e, c=c: e.tensor_scalar(out=wrot.ap[:, :, c:c + 8], in0=wA.ap[:, :, c + 8:c + 16], scalar1=-1.0, scalar2=None, op0=ALU.mult), [wrot], [wA])
                    OP(k, "dve", lambda e, c=c: e.tensor_copy(out=wrot.ap[:, :, c + 8:c + 16], in_=wA.ap[:, :, c:c + 8]), [wrot], [wA])

        if stage > 0:
            load_w_A(0)
        for hA in range(8 if stage > 0 else 0):
            for g in range(NQG if stage > 1 else 0):
                ts = slice(g * 512, (g + 1) * 512)
                for j in range(2):
                    ba, br = ps[2 * j], ps[2 * j + 1]
                    MMG(k, ba, [(ba.ap, wA.ap[:, kc, j * 128:(j + 1) * 128], hT.ap[:, kc, ts]) for kc in range(8)], [wA, hT])
                    MMG(k, br, [(br.ap, wrot.ap[:, kc, j * 128:(j + 1) * 128], hT.ap[:, kc, ts]) for kc in range(8)], [wrot, hT])
                    OP(k, "dve", lambda e, ba=ba: e.tensor_tensor(out=t1.ap[:], in0=ba.ap, in1=cosT.ap[:, ts], op=ALU.mult), [t1], [ba, cosT])
                    OP(k, "dve", lambda e, br=br: e.tensor_tensor(out=t2.ap[:], in0=br.ap, in1=sinT.ap[:, ts], op=ALU.mult), [t2], [br, sinT])
                    if j == 0:
                        OP(k, "dve", lambda e: e.tensor_tensor(out=qT.ap[:, ts], in0=t1.ap[:], in1=t2.ap[:], op=ALU.add), [qT], [t1, t2])
                    else:
                        OP(k, "dve", lambda e: e.tensor_tensor(out=t1.ap[:], in0=t1.ap[:], in1=t2.ap[:], op=ALU.add), [t1], [t1, t2])
                        for m in range(2):
                            hs_ = slice(m * 64, m * 64 + 64)
                            OP(k, "act", lambda e, m=m, hs_=hs_: e.copy(out=kz[m].ap[hs_, ts], in_=t1.ap[hs_, :]), [kz[m]], [t1])
            for tb in range(NTB if stage > 2 else 0):
                bv = ps[4 + tb % 2]
                MMG(k, bv, [(bv.ap[:, 0:128], hT.ap[:, kc, tb * 128:(tb + 1) * 128], wA.ap[:, kc, 256:384]) for kc in range(8)], [wA, hT])
                OP(k, "act" if tb % 2 else "dve", (lambda e, tb=tb, bv=bv: e.copy(out=V.ap[:, tb, :], in_=bv.ap[:, 0:128])) if tb % 2 else
                   (lambda e, tb=tb, bv=bv: e.tensor_copy(out=V.ap[:, tb, :], in_=bv.ap[:, 0:128])), [V], [bv])
            if hA + 1 < 8:
                load_w_A(hA + 1)
            jobs = []
            for qg in range(NQG if stage > 3 else 0):
                for m in range(2):
                    nkb = 4 * qg + 4
                    for kb in range(nkb):
                        jobs.append(dict(qg=qg, m=m, kb=kb, nkb=nkb, c0=max(0, kb - 4 * qg) * 128, i=len(jobs), last_qg=(m == 1 and kb == nkb - 1)))

            def a1(J):
                qg, m, kb, c0, i = J["qg"], J["m"], J["kb"], J["c0"], J["i"]
                sc, p = ps[i % 3], pT[i % 4]
                MMG(k, sc, [(sc.ap[:, c0:512], kz[m].ap[:, kb * 128:(kb + 1) * 128], qT.ap[:, qg * 512 + c0:(qg + 1) * 512])], [kz[m], qT])
                OP(k, "act", lambda e: e.activation(out=p.ap[:, c0:512], in_=sc.ap[:, c0:512], func=AF.Exp, scale=0.125), [p], [sc])
                if kb >= 4 * qg:
                    OP(k, "pool", lambda e: e.tensor_tensor(out=p.ap[:, c0:c0 + 128], in0=p.ap[:, c0:c0 + 128], in1=mskA.ap[:], op=ALU.mult), [p], [mskA])

            def a2(J):
                qg, m, kb, c0, i, nkb = J["qg"], J["m"], J["kb"], J["c0"], J["i"], J["nkb"]
                p = pT[i % 4]
                ao, asum = ps[4 + m], ps[6 + m]
                MMG(k, ao, [(ao.ap[:, c0:512], V.ap[:, kb, :], p.ap[:, c0:512])], [V, p], first=(kb == 0), last=(kb == nkb - 1))
                MMG(k, asum, [(asum.ap[:, c0:512], ones.ap[:], p.ap[:, c0:512])], [ones, p], first=(kb == 0), last=(kb == nkb - 1))
                if J["last_qg"]:
                    epi(qg)

            def epi(qg):
                OP(k, "dve", lambda e: e.reciprocal(out=e0.ap[:], in_=ps[6].ap), [e0], [ps[6]])
                OP(k, "dve", lambda e: e.tensor_tensor(out=e0.ap[:], in0=ps[4].ap, in1=e0.ap[:], op=ALU.mult), [e0], [ps[4], e0])
                OP(k, "dve", lambda e: e.reciprocal(out=e1.ap[:], in_=ps[7].ap), [e1], [ps[7]])
                OP(k, "dve", lambda e: e.tensor_tensor(out=e1.ap[:], in0=ps[5].ap, in1=e1.ap[:], op=ALU.mult), [e1], [ps[5], e1])
                OP(k, "dve", lambda e: e.scalar_tensor_tensor(out=e2.ap[:], in0=e1.ap[:], scalar=neglam.ap[:, 0:1], in1=e0.ap[:], op0=ALU.mult, op1=ALU.add), [e2], [e0, e1, neglam])
                OP(k, "act", lambda e: e.activation(out=sq.ap[:], in_=e2.ap[:], func=AF.Square), [sq], [e2])
                MMG(k, ps[3], [(ps[3].ap, ones.ap[:], sq.ap[:])], [ones, sq])
                OP(k, "act", lambda e: e.activation(out=e0.ap[:], in_=ps[3].ap, func=AF.Ln, bias=LN_EPS_AP[0][:, 0:1], scale=1.0 / 128), [e0], [ps[3]], extra=[LN_EPS_AP[1]])
                OP(k, "act", lambda e: e.activation(out=e0.ap[:], in_=e0.ap[:], func=AF.Exp, scale=-0.5), [e0], [e0])
                OP(k, "dve", lambda e: e.tensor_tensor(out=e2.ap[:], in0=e2.ap[:], in1=e0.ap[:], op=ALU.mult), [e2], [e0, e2])
                o = oo[qg % 2]
                OP(k, "pool", lambda e, o=o: e.tensor_scalar(out=o.ap[:], in0=e2.ap[:], scalar1=gsub.ap[:, 0:1], scalar2=None, op0=ALU.mult), [o], [e2, gsub])
                DMA(k, "sp", oaT_d[hA * 128:(hA + 1) * 128, qg * 512:(qg + 1) * 512], o.ap[:], d_o[qg % 2], [], [o])

            run_pipeline(jobs, (a1, a2), [0, 2])
        k.barrier()


def emit_stickbreak(k, hT_d, ocT_d, w_mix, cst, S, stage=9, hT_res=None):
    nc = k.nc
    k.begin_phase()
    NQG = S // 512
    NTB = S // 128
    with ExitStack() as es:
        sb = lambda name, shape, dt: TB(es.enter_context(nc.sbuf_tensor(uname(name), shape, dt)))
        hT = hT_res if hT_res is not None else sb("hT", [128, 8, S], BF16)
        qT = sb("qT", [128, S], BF16)
        kz = [sb("kz%d" % i, [128, S], BF16) for i in range(2)]
        Vz = [sb("Vz%d" % i, [128, NTB, 128], BF16) for i in range(2)]
        wst = sb("wst", [128, 8, 384], F32)
        wA = sb("wA", [128, 8, 384], BF16)
        e1b = [sb("e1b%d" % i, [128, 512], F32) for i in range(5)]
        spb = [sb("spb%d" % i, [128, 512], BF16) for i in range(4)]
        t1b = [sb("t1b%d" % i, [128, 512], F32) for i in range(3)]
        ab = [sb("ab%d" % i, [128, 512], BF16) for i in range(3)]
        Rs2 = [sb("Rs%d" % i, [128, 512], F32) for i in range(2)]
        oc = [sb("oc%d" % i, [128, 512], BF16) for i in range(2)]
        ones = sb("ones", [128, 128], BF16)
        LT = sb("LT", [128, 128], BF16)
        mS = sb("mS", [128, 128], BF16)
        zer = sb("zer", [128, 128], BF16)
        psum = es.enter_context(nc.psum_tensor(uname("ps"), [128, 8, 512], F32))
        ps = [TB(psum[:, i, :]) for i in range(8)]
        d_h = k.dsem("hT")
        d_c = k.dsem("c")
        d_w = k.dsem("w")
        d_o = [k.dsem("o") for _ in range(2)]
        hv = hT_d.rearrange("(c p) s -> p c s", p=128)
        if hT_res is None or hT_res.w is None:
            DMAG(k, "sp", [(hT.ap[:, kc, :], hv[:, kc, :]) for kc in range(8)], d_h, [hT], [])
        DMAG(k, "sp", [(LT.ap[:], cst["LT"]), (mS.ap[:], cst["mS"])], d_c, [LT, mS], [])
        OP(k, "pool", lambda e: e.memset(ones.ap[:], 1.0), [ones], [])
        OP(k, "pool", lambda e: e.memset(zer.ap[:], 0.0), [zer], [])
        for i in range(2):
            OP(k, "pool", lambda e, i=i: e.memset(kz[i].ap[:], 0.0), [kz[i]], [])
            OP(k, "pool", lambda e, i=i: e.memset(Vz[i].ap[:], 0.0), [Vz[i]], [])
        it = 0
        def load_w_C(hp):
            DMAG(k, "sp", [(wst.ap[:, :, j * 128:(j + 1) * 128], w_mix[:, off + hp * 128: off + (hp + 1) * 128].rearrange("(c p) n -> p c n", p=128))
                           for j, off in enumerate((OFF_QC, OFF_KC, OFF_VC))], d_w, [wst], [])
            OP(k, "pool", lambda e: e.tensor_copy(out=wA.ap[:], in_=wst.ap[:]), [wA], [wst])

        load_w_C(0)
        for hp in range(8):
            for g in range(NQG):
                ts = slice(g * 512, (g + 1) * 512)
                for j in range(2):
                    ba = ps[j]
                    MMG(k, ba, [(ba.ap, wA.ap[:, kc, j * 128:(j + 1) * 128], hT.ap[:, kc, ts]) for kc in range(8)], [wA, hT])
                    if j == 0:
                        OP(k, "act", lambda e, ba=ba: e.copy(out=qT.ap[:, ts], in_=ba.ap), [qT], [ba])
                    else:
                        for m in range(2):
                            hs_ = slice(m * 64, m * 64 + 64)
                            OP(k, "act", lambda e, ba=ba, m=m, hs_=hs_: e.copy(out=kz[m].ap[hs_, ts], in_=ba.ap[hs_, :]), [kz[m]], [ba])
            for tb in range(NTB if stage > 1 else 0):
                bv = ps[2 + tb % 2]
                MMG(k, bv, [(bv.ap[:, 0:128], hT.ap[:, kc, tb * 128:(tb + 1) * 128], wA.ap[:, kc, 256:384]) for kc in range(8)], [wA, hT])
                if tb % 2 == 0:
                    OP(k, "act", lambda e, tb=tb, bv=bv: e.copy(out=Vz[0].ap[:, tb, 0:64], in_=bv.ap[:, 0:64]), [Vz[0]], [bv])
                    OP(k, "act", lambda e, tb=tb, bv=bv: e.copy(out=Vz[1].ap[:, tb, 64:128], in_=bv.ap[:, 64:128]), [Vz[1]], [bv])
                else:
                    OP(k, "dve", lambda e, tb=tb, bv=bv: e.tensor_copy(out=Vz[0].ap[:, tb, 0:64], in_=bv.ap[:, 0:64]), [Vz[0]], [bv])
                    OP(k, "dve", lambda e, tb=tb, bv=bv: e.tensor_copy(out=Vz[1].ap[:, tb, 64:128], in_=bv.ap[:, 64:128]), [Vz[1]], [bv])
            if hp + 1 < 8:
                load_w_C(hp + 1)
            jobs = []
            for qg in range(NQG if stage > 2 else 0):
                nkb = 4 * qg + 4
                for h2 in range(2):
                    for kb in range(nkb - 1, -1, -1):
                        jobs.append(dict(qg=qg, h2=h2, kb=kb, c0=max(0, kb - 4 * qg) * 128, i=len(jobs),
                                         first_seq=(kb == nkb - 1), first_qg=(h2 == 0 and kb == nkb - 1), last_qg=(h2 == 1 and kb == 0)))

            def s1(J):
                qg, h2, kb, c0, i = J["qg"], J["h2"], J["kb"], J["c0"], J["i"]
                cs = slice(c0, 512)
                z, e1 = ps[i % 3], e1b[i % 5]
                OP(k, "act", lambda e: e.activation(out=e1.ap[:, cs], in_=z.ap[:, cs], func=AF.Exp, scale=0.125), [e1], [z])

            def s0(J):
                qg, h2, kb, c0, i = J["qg"], J["h2"], J["kb"], J["c0"], J["i"]
                cs = slice(c0, 512)
                z = ps[i % 3]
                MMG(k, z, [(z.ap[:, cs], kz[h2].ap[:, kb * 128:(kb + 1) * 128], qT.ap[:, qg * 512 + c0:(qg + 1) * 512])], [kz[h2], qT])

            def s1b(J):
                qg, kb, c0, i = J["qg"], J["kb"], J["c0"], J["i"]
                cs = slice(c0, 512)
                e1, sp = e1b[i % 5], spb[i % 4]
                OP(k, "act", lambda e: e.activation(out=sp.ap[:, cs], in_=e1.ap[:, cs], func=AF.Ln, bias=1.0, scale=1.0), [sp], [e1], extra=[ONE_AP[1]])
                if kb >= 4 * qg:
                    OP(k, "pool", lambda e: e.tensor_tensor(out=sp.ap[:, c0:c0 + 128], in0=sp.ap[:, c0:c0 + 128], in1=mS.ap[:], op=ALU.mult), [sp], [mS])
                    OP(k, "pool", lambda e: e.tensor_tensor(out=e1.ap[:, c0:c0 + 128], in0=e1.ap[:, c0:c0 + 128], in1=mS.ap[:], op=ALU.mult), [e1], [mS])

            def s2(J):
                c0, i = J["c0"], J["i"]
                cs = slice(c0, 512)
                sp, t1 = spb[i % 4], t1b[i % 3]
                Cb, Sb = ps[3 + i % 2], ps[5 + i % 2]
                Rc, Rn = Rs2[i % 2], Rs2[(i + 1) % 2]
                if J["first_seq"]:
                    OP(k, "pool", lambda e: e.memset(Rc.ap[:], 0.0), [Rc], [])
                    OP(k, "pool", lambda e: e.memset(Rn.ap[:], 0.0), [Rn], [])
                MMG(k, Cb, [(Cb.ap[:, cs], LT.ap[:], sp.ap[:, cs])], [LT, sp])
                MMG(k, Sb, [(Sb.ap[:, cs], ones.ap[:], sp.ap[:, cs])], [ones, sp])
                OP(k, "dve", lambda e: e.tensor_tensor(out=t1.ap[:, cs], in0=Cb.ap[:, cs], in1=Rc.ap[:, cs], op=ALU.add), [t1], [Cb, Rc])
                OP(k, "dve", lambda e: e.tensor_tensor(out=Rn.ap[:, cs], in0=Sb.ap[:, cs], in1=Rc.ap[:, cs], op=ALU.add), [Rn], [Sb, Rc])

            def s2b(J):
                c0, i = J["c0"], J["i"]
                cs = slice(c0, 512)
                e1, t1, a = e1b[i % 5], t1b[i % 3], ab[i % 3]
                OP(k, "act", lambda e: e.activation(out=t1.ap[:, cs], in_=t1.ap[:, cs], func=AF.Exp, scale=-1.0), [t1], [t1])
                OP(k, "dve" if i % 4 == 3 else "pool", lambda e: e.tensor_tensor(out=a.ap[:, cs], in0=e1.ap[:, cs], in1=t1.ap[:, cs], op=ALU.mult), [a], [e1, t1])

            def s3(J):
                qg, h2, kb, c0, i = J["qg"], J["h2"], J["kb"], J["c0"], J["i"]
                cs = slice(c0, 512)
                a = ab[i % 3]
                acc = ps[7]
                if J["first_qg"]:
                    MMG(k, acc, [(acc.ap, zer.ap[:], qT.ap[:, 0:512])], [zer, qT], first=True, last=False)
                MMG(k, acc, [(acc.ap[:, cs], Vz[h2].ap[:, kb, :], a.ap[:, cs])], [Vz[h2], a], first=False, last=J["last_qg"])
                if J["last_qg"]:
                    o = oc[qg % 2]
                    OP(k, "act", lambda e: e.copy(out=o.ap[:], in_=acc.ap), [o], [acc])
                    DMA(k, "sp", ocT_d[hp * 128:(hp + 1) * 128, qg * 512:(qg + 1) * 512], o.ap[:], d_o[qg % 2], [], [o])

            run_pipeline(jobs, (s0, s1, s2, s2b, s1b, s3), [0, 1, 3, 4, 1, 5])
        k.barrier()


ONE_AP = [None, None]


def emit_mamba(k, hT_d, obT_d, w_mix, convw_d, convb_d, dtb_d, alog_d, dsk_d, ng_d, cst, S, hT_res=None):
    nc = k.nc
    k.begin_phase()
    NG = S // 512
    with ExitStack() as es:
        sb = lambda name, shape, dt: TB(es.enter_context(nc.sbuf_tensor(uname(name), shape, dt)))
        hT = hT_res if hT_res is not None else sb("hT", [128, 8, S], BF16)
        wz = sb("wz", [128, 8, 1024], BF16)
        wx = sb("wx", [128, 8, 1536], BF16)
        wdt = sb("wdt", [128, 8, 16], BF16)
        wst = [sb("wst%d" % i, [128, 8, 256], F32) for i in range(2)]
        wst16 = sb("wst16", [128, 8, 16], F32)
        hist = sb("hist", [128, 12, 3], F32)
        rw = [sb("rw%d" % i, [128, 515], F32) for i in range(2)]
        cacc = [sb("cacc%d" % i, [128, 512], F32) for i in range(2)]
        xbcT = sb("xbcT", [128, 12, 512], BF16)
        xs_tok = sb("xs_tok", [128, 4, 1024], BF16)
        B_tok = sb("B_tok", [128, 4, 256], BF16)
        dtp = sb("dtp", [128, 16], F32)
        dt = sb("dt", [128, 16], F32)
        dtA = sb("dtA", [128, 16], F32)
        ex = sb("ex", [128, 48], F32)
        xdt = sb("xdt", [128, 16, 64], BF16)
        xdte = sb("xdte", [128, 16, 64], BF16)
        cbs = sb("cbs", [128, 2, 128], F32)
        cbm = sb("cbm", [128, 2, 128], F32)
        A4 = [sb("A4%d" % i, [128, 4, 128], F32) for i in range(2)]
        dec = [sb("dec%d" % i, [128, 4, 128], F32) for i in range(2)]
        M4 = [sb("M4%d" % i, [128, 4, 128], BF16) for i in range(2)]
        zs = [sb("zs%d" % i, [128, 512], F32) for i in range(2)]
        yo = [sb("yo%d" % i, [128, 512], F32) for i in range(2)]
        yg = [sb("yg%d" % i, [128, 512], F32) for i in range(2)]
        dx = sb("dx", [128, 512], F32)
        junk = sb("junk", [128, 512], F32)
        ssq = sb("ssq", [128, 2], F32)
        rs2 = sb("rs2", [128, 2], F32)
        ob = sb("ob", [128, 1024], BF16)
        obT = [sb("obT%d" % i, [128, 8, 128], BF16) for i in range(2)]
        H = [sb("H%d" % i, [128, 512], F32) for i in range(2)]
        Hbf = [sb("Hbf%d" % i, [128, 512], BF16) for i in range(2)]
        TRI = sb("TRI", [128, 128], F32)
        SGT = sb("SGT", [128, 128], F32)
        MLE = sb("MLE", [128, 128], F32)
        ones32 = sb("ones32", [128, 128], F32)
        idb = sb("idb", [128, 128], BF16)
        convw = sb("convw", [128, 12, 4], F32)
        convb = sb("convb", [128, 12], F32)
        dtbias = sb("dtbias", [128, 16], F32)
        aneg = sb("aneg", [128, 16], F32)
        dskip = sb("dskip", [128, 16], F32)
        normg = sb("normg", [128, 1024], F32)
        psum = es.enter_context(nc.psum_tensor(uname("ps"), [128, 7, 512], F32))
        ps = [TB(psum[:, i, :]) for i in range(7)]
        psT = TB(es.enter_context(nc.psum_tensor(uname("psT"), [128, 1024], BF16)))
        d_h, d_c, d_o = k.dsem("hT"), k.dsem("c"), [k.dsem("o") for _ in range(2)]
        d_w = [k.dsem("w") for _ in range(2)]
        d_w16 = k.dsem("w16")

        hv = hT_d.rearrange("(c p) s -> p c s", p=128)
        if hT_res is None or hT_res.w is None:
            DMAG(k, "sp", [(hT.ap[:, kc, :], hv[:, kc, :]) for kc in range(8)], d_h, [hT], [])
        DMAG(k, "sp", [(TRI.ap[:], cst["TRI"]), (SGT.ap[:], cst["SGT"]), (MLE.ap[:], cst["MLE"]), (idb.ap[:], cst["ident"]),
                       (convw.ap[:], convw_d), (convb.ap[:], convb_d), (dtbias.ap[:], dtb_d.partition_broadcast(128)),
                       (aneg.ap[:], alog_d.partition_broadcast(128)), (dskip.ap[:], dsk_d.partition_broadcast(128)),
                       (normg.ap[:], ng_d.partition_broadcast(128))], d_c,
             [TRI, SGT, MLE, idb, convw, convb, dtbias, aneg, dskip, normg], [])
        OP(k, "pool", lambda e: e.memset(ones32.ap[:], 1.0), [ones32], [])
        OP(k, "pool", lambda e: e.memset(hist.ap[:], 0.0), [hist], [])
        for i in range(2):
            OP(k, "pool", lambda e, i=i: e.memset(H[i].ap[:], 0.0), [H[i]], [])
            OP(k, "pool", lambda e, i=i: e.memset(Hbf[i].ap[:], 0.0), [Hbf[i]], [])
        OP(k, "act", lambda e: e.activation(out=aneg.ap[:], in_=aneg.ap[:], func=AF.Exp), [aneg], [aneg])
        OP(k, "dve", lambda e: e.tensor_scalar(out=aneg.ap[:], in0=aneg.ap[:], scalar1=-1.0, scalar2=None, op0=ALU.mult), [aneg], [aneg])
        wi = 0
        for dst, off, n in ((wz, OFF_ZB, 1024), (wx, OFF_XBC, 1536)):
            for c0 in range(0, n, 256):
                s_ = wst[wi % 2]
                DMAG(k, "sp", [(s_.ap[:], w_mix[:, off + c0: off + c0 + 256].rearrange("(c p) n -> p c n", p=128))], d_w[wi % 2], [s_], [])
                OP(k, "pool" if wi % 2 else "dve", lambda e, dst=dst, c0=c0, s_=s_: e.tensor_copy(out=dst.ap[:, :, c0:c0 + 256], in_=s_.ap[:]), [dst], [s_])
                wi += 1
        DMAG(k, "sp", [(wst16.ap[:], w_mix[:, OFF_DT:OFF_DT + 16].rearrange("(c p) n -> p c n", p=128))], d_w16, [wst16], [])
        OP(k, "dve", lambda e: e.tensor_copy(out=wdt.ap[:], in_=wst16.ap[:]), [wdt], [wst16])

        ci = 0
        for g in range(NG):
            gs = slice(g * 512, (g + 1) * 512)
            for cc in range(12):
                bk = ps[cc % 2]
                r_, ca = rw[cc % 2], cacc[cc % 2]
                MMG(k, bk, [(bk.ap, wx.ap[:, kc, cc * 128:(cc + 1) * 128], hT.ap[:, kc, gs]) for kc in range(8)], [wx, hT])
                OP(k, "pool", lambda e, r_=r_, cc=cc: e.tensor_copy(out=r_.ap[:, 0:3], in_=hist.ap[:, cc, :]), [r_], [hist])
                OP(k, "act", lambda e, r_=r_, bk=bk: e.copy(out=r_.ap[:, 3:515], in_=bk.ap), [r_], [bk])
                OP(k, "pool", lambda e, r_=r_, cc=cc: e.tensor_copy(out=hist.ap[:, cc, :], in_=r_.ap[:, 512:515]), [hist], [r_])
                OP(k, "dve", lambda e, r_=r_, ca=ca, cc=cc: e.tensor_scalar(out=ca.ap[:], in0=r_.ap[:, 0:512], scalar1=convw.ap[:, cc, 0:1], scalar2=None, op0=ALU.mult), [ca], [r_, convw])
                for j in range(1, 4):
                    OP(k, "dve", lambda e, r_=r_, ca=ca, cc=cc, j=j: e.scalar_tensor_tensor(out=ca.ap[:], in0=r_.ap[:, j:j + 512], scalar=convw.ap[:, cc, j:j + 1], in1=ca.ap[:], op0=ALU.mult, op1=ALU.add), [ca], [r_, convw])
                OP(k, "act", lambda e, ca=ca, cc=cc: e.activation(out=xbcT.ap[:, cc, :], in_=ca.ap[:], func=AF.Silu, bias=convb.ap[:, cc:cc + 1], scale=1.0), [xbcT], [ca, convb])
            for tb4 in range(4):
                ls = slice(tb4 * 128, (tb4 + 1) * 128)
                MMT = []
                deps = _deps_of([psT], [xbcT, idb], [])
                for cc in range(8):
                    tk = k.op("pe", lambda e, cc=cc: e.transpose(psT.ap[:, cc * 128:(cc + 1) * 128], xbcT.ap[:, cc, ls], idb.ap[:]), deps if cc == 0 else (), sig=(cc == 7))
                _mark(tk, [psT], [xbcT, idb])
                OP(k, "act", lambda e, tb4=tb4: e.copy(out=xs_tok.ap[:, tb4, :], in_=psT.ap[:]), [xs_tok], [psT])
                deps = _deps_of([psT], [xbcT, idb], [])
                for cc in range(2):
                    tk = k.op("pe", lambda e, cc=cc: e.transpose(psT.ap[:, cc * 128:(cc + 1) * 128], xbcT.ap[:, 8 + cc, ls], idb.ap[:]), deps if cc == 0 else (), sig=(cc == 1))
                _mark(tk, [psT], [xbcT, idb])
                OP(k, "act", lambda e, tb4=tb4: e.copy(out=B_tok.ap[:, tb4, :], in_=psT.ap[:, 0:256]), [B_tok], [psT])
            for tb4 in range(4):
                c = g * 4 + tb4
                T = slice(c * 128, (c + 1) * 128)
                ls = slice(tb4 * 128, (tb4 + 1) * 128)
                sm = ps[6]
                MMG(k, sm, [(sm.ap[:, 0:16], hT.ap[:, kc, T], wdt.ap[:, kc, :]) for kc in range(8)], [hT, wdt])
                OP(k, "dve", lambda e: e.tensor_tensor(out=dtp.ap[:], in0=sm.ap[:, 0:16], in1=dtbias.ap[:], op=ALU.add), [dtp], [sm, dtbias])
                OP(k, "act", lambda e: e.activation(out=dtp.ap[:], in_=dtp.ap[:], func=AF.Exp), [dtp], [dtp])
                OP(k, "act", lambda e: e.activation(out=dt.ap[:], in_=dtp.ap[:], func=AF.Ln, bias=1.0, scale=1.0), [dt], [dtp], extra=[ONE_AP[1]])
                OP(k, "dve", lambda e: e.tensor_tensor(out=dtA.ap[:], in0=dt.ap[:], in1=aneg.ap[:], op=ALU.mult), [dtA], [dt, aneg])
                for i, m_ in enumerate((TRI, SGT, ones32)):
                    MMG(k, sm, [(sm.ap[:, 16 + i * 16:32 + i * 16], m_.ap[:], dtA.ap[:])], [m_, dtA], first=False, start=True)
                OP(k, "act", lambda e: e.activation(out=ex.ap[:], in_=sm.ap[:, 16:64], func=AF.Exp), [ex], [sm])
                xs3 = xs_tok.ap[:, tb4, :].rearrange("p (h d) -> p h d", d=64)
                OP(k, "pool", lambda e: e.tensor_tensor(out=xdt.ap[:], in0=xs3, in1=dt.ap[:].unsqueeze(2).to_broadcast([128, 16, 64]), op=ALU.mult), [xdt], [xs_tok, dt])
                OP(k, "pool", lambda e: e.tensor_tensor(out=xdte.ap[:], in0=xdt.ap[:], in1=ex.ap[:, 16:32].unsqueeze(2).to_broadcast([128, 16, 64]), op=ALU.mult), [xdte], [xdt, ex])
                for gg in range(2):
                    MMG(k, ps[3], [(ps[3].ap[:, gg * 128:128 + gg * 128], xbcT.ap[:, 8 + gg, ls], xbcT.ap[:, 10 + gg, ls])], [xbcT], first=(gg == 0), start=True)
                OP(k, "act", lambda e: e.copy(out=cbs.ap[:].rearrange("p a b -> p (a b)"), in_=ps[3].ap[:, 0:256]), [cbs], [ps[3]])
                OP(k, "pool", lambda e: e.tensor_tensor(out=cbm.ap[:], in0=cbs.ap[:], in1=MLE.ap[:].unsqueeze(1).to_broadcast([128, 2, 128]), op=ALU.mult), [cbm], [cbs, MLE])
                for gg in range(2):
                    G = slice(gg * 512, (gg + 1) * 512)
                    MMG(k, ps[0], [(ps[0].ap, hT.ap[:, kc, T], wz.ap[:, kc, G]) for kc in range(8)], [hT, wz])
                    OP(k, "act", lambda e, gg=gg: e.activation(out=zs[gg].ap[:], in_=ps[0].ap, func=AF.Silu), [zs[gg]], [ps[0]])
                    MMG(k, ps[1], [(ps[1].ap, xbcT.ap[:, 10 + gg, ls], Hbf[gg].ap[:])], [xbcT, Hbf[gg]])
                    OP(k, "act", lambda e, gg=gg: e.copy(out=yo[gg].ap[:], in_=ps[1].ap), [yo[gg]], [ps[1]])
                    for r in range(2):
                        i2 = ci % 2
                        ci += 1
                        a4, dc, m4, sg_ = A4[i2], dec[i2], M4[i2], ps[4 + i2]
                        for i in range(4):
                            h = gg * 8 + r * 4 + i
                            OP(k, "dve", lambda e, a4=a4, i=i, h=h: e.tensor_scalar(out=a4.ap[:, i, :], in0=SGT.ap[:], scalar1=dtA.ap[:, h:h + 1], scalar2=None, op0=ALU.mult), [a4], [SGT, dtA])
                        for i in range(4):
                            MMG(k, sg_, [(sg_.ap[:, i * 128:(i + 1) * 128], a4.ap[:, i, :], TRI.ap[:])], [a4, TRI], first=(i == 0), start=True)
                        OP(k, "act", lambda e, dc=dc, sg_=sg_: e.activation(out=dc.ap[:].rearrange("p a b -> p (a b)"), in_=sg_.ap, func=AF.Exp), [dc], [sg_])
                        OP(k, "pool", lambda e, dc=dc, m4=m4, gg=gg: e.tensor_tensor(out=m4.ap[:], in0=dc.ap[:], in1=cbm.ap[:, gg, :].unsqueeze(1).to_broadcast([128, 4, 128]), op=ALU.mult), [m4], [dc, cbm])
                        for i in range(4):
                            h = gg * 8 + r * 4 + i
                            hh = r * 4 + i
                            MMG(k, ps[2], [(ps[2].ap[:, hh * 64:(hh + 1) * 64], m4.ap[:, i, :], xdt.ap[:, h, :])], [m4, xdt], first=(r == 0 and i == 0), start=True)
                    y_ = yg[gg]
                    y3 = y_.ap[:].rearrange("p (h d) -> p h d", d=64)
                    OP(k, "dve", lambda e, gg=gg, y3=y3: e.tensor_tensor(out=y3, in0=yo[gg].ap[:].rearrange("p (h d) -> p h d", d=64),
                                                                   in1=ex.ap[:, gg * 8:(gg + 1) * 8].unsqueeze(2).to_broadcast([128, 8, 64]), op=ALU.mult), [y_], [yo[gg], ex])
                    OP(k, "dve", lambda e, y_=y_: e.tensor_tensor(out=y_.ap[:], in0=ps[2].ap, in1=y_.ap[:], op=ALU.add), [y_], [ps[2]])
                    OP(k, "pool", lambda e, gg=gg: e.tensor_tensor(out=dx.ap[:].rearrange("p (h d) -> p h d", d=64), in0=xs_tok.ap[:, tb4, gg * 512:(gg + 1) * 512].rearrange("p (h d) -> p h d", d=64),
                                                               in1=dskip.ap[:, gg * 8:(gg + 1) * 8].unsqueeze(2).to_broadcast([128, 8, 64]), op=ALU.mult), [dx], [xs_tok, dskip])
                    OP(k, "pool", lambda e, y_=y_: e.tensor_tensor(out=y_.ap[:], in0=y_.ap[:], in1=dx.ap[:], op=ALU.add), [y_], [dx])
                    OP(k, "dve", lambda e, y_=y_, gg=gg: e.tensor_tensor(out=y_.ap[:], in0=y_.ap[:], in1=zs[gg].ap[:], op=ALU.mult), [y_], [zs[gg]])
                    OP(k, "act", lambda e, y_=y_, gg=gg: e.activation(out=junk.ap[:], in_=y_.ap[:], func=AF.Square, accum_out=ssq.ap[:, gg:gg + 1]), [junk, ssq], [y_])
                    MMG(k, ps[3], [(ps[3].ap, B_tok.ap[:, tb4, gg * 128:(gg + 1) * 128], xdte.ap[:, gg * 8:(gg + 1) * 8, :].rearrange("p h d -> p (h d)"))], [B_tok, xdte])
                    H3 = H[gg].ap[:].rearrange("p (h d) -> p h d", d=64)
                    OP(k, "dve", lambda e, H3=H3, gg=gg: e.tensor_tensor(out=H3, in0=H3, in1=ex.ap[:, 32 + gg * 8:40 + gg * 8].unsqueeze(2).to_broadcast([128, 8, 64]), op=ALU.mult), [H[gg]], [ex])
                    OP(k, "dve", lambda e, gg=gg: e.tensor_tensor(out=H[gg].ap[:], in0=ps[3].ap, in1=H[gg].ap[:], op=ALU.add), [H[gg]], [ps[3]])
                    OP(k, "act", lambda e, gg=gg: e.copy(out=Hbf[gg].ap[:], in_=H[gg].ap[:]), [Hbf[gg]], [H[gg]])
                OP(k, "dve", lambda e: e.tensor_scalar(out=rs2.ap[:], in0=ssq.ap[:], scalar1=1.0 / 512, scalar2=LN_EPS, op0=ALU.mult, op1=ALU.add), [rs2], [ssq])
                OP(k, "act", lambda e: e.activation(out=rs2.ap[:], in_=rs2.ap[:], func=AF.Ln), [rs2], [rs2])
                OP(k, "act", lambda e: e.activation(out=rs2.ap[:], in_=rs2.ap[:], func=AF.Exp, scale=-0.5), [rs2], [rs2])
                for gg in range(2):
                    G = slice(gg * 512, (gg + 1) * 512)
                    OP(k, "dve", lambda e, gg=gg, G=G: e.scalar_tensor_tensor(out=ob.ap[:, G], in0=yg[gg].ap[:], scalar=rs2.ap[:, gg:gg + 1], in1=normg.ap[:, G], op0=ALU.mult, op1=ALU.mult), [ob], [yg[gg], rs2, normg])
                deps = _deps_of([psT], [ob, idb], [])
                for cc in range(8):
                    tk = k.op("pe", lambda e, cc=cc: e.transpose(psT.ap[:, cc * 128:(cc + 1) * 128], ob.ap[:, cc * 128:(cc + 1) * 128], idb.ap[:]), deps if cc == 0 else (), sig=(cc == 7))
                _mark(tk, [psT], [ob, idb])
                o_ = obT[c % 2]
                OP(k, "act", lambda e, o_=o_: e.copy(out=o_.ap[:].rearrange("p a b -> p (a b)"), in_=psT.ap[:]), [o_], [psT])
                DMA(k, "sp", obT_d.rearrange("(c p) s -> p c s", p=128)[:, :, T], o_.ap[:], d_o[c % 2], [], [o_])
        k.barrier()


def emit_transpose(k, x_d, hT_d, ident, S):
    nc = k.nc
    k.begin_phase()
    with ExitStack() as es:
        sb = lambda name, shape, dt: TB(es.enter_context(nc.sbuf_tensor(uname(name), shape, dt)))
        xin = [sb("xin%d" % i, [128, D], F32) for i in range(2)]
        xb = [sb("xb%d" % i, [128, D], BF16) for i in range(2)]
        xo = [sb("xo%d" % i, [128, 8, 128], BF16) for i in range(2)]
        idb = sb("idb", [128, 128], BF16)
        psT = [TB(es.enter_context(nc.psum_tensor(uname("psT"), [128, 1024], BF16))) for _ in range(2)]
        d_c = k.dsem("c")
        d_x = [k.dsem("x") for _ in range(2)]
        d_o = [k.dsem("o") for _ in range(2)]
        DMAG(k, "sp", [(idb.ap[:], ident)], d_c, [idb], [])
        hv = hT_d.rearrange("(c p) s -> p c s", p=128)
        for t in range(S // 128):
            s = t % 2
            DMAG(k, "sp", [(xin[s].ap[:], x_d[t * 128:(t + 1) * 128, :])], d_x[s], [xin[s]], [])
            OP(k, "act", lambda e, s=s: e.copy(out=xb[s].ap[:], in_=xin[s].ap[:]), [xb[s]], [xin[s]])
            deps = _deps_of([psT[s]], [xb[s], idb], [])
            for cc in range(8):
                tk = k.op("pe", lambda e, cc=cc, s=s: e.transpose(psT[s].ap[:, cc * 128:(cc + 1) * 128], xb[s].ap[:, cc * 128:(cc + 1) * 128], idb.ap[:]), deps if cc == 0 else (), sig=(cc == 7))
            _mark(tk, [psT[s]], [xb[s], idb])
            OP(k, "dve", lambda e, s=s: e.tensor_copy(out=xo[s].ap[:].rearrange("p a b -> p (a b)"), in_=psT[s].ap[:]), [xo[s]], [psT[s]])
            DMA(k, "sp", hv[:, :, t * 128:(t + 1) * 128], xo[s].ap[:], d_o[s], [], [xo[s]])
        k.barrier()


def emit_merge(k, x_d, hT_d, oT_ds, w_mix, gbias_d, w_br, w_o, g, b, x_out, S):
    nc = k.nc
    k.begin_phase()
    TG = 256
    NG = S // TG
    with ExitStack() as es:
        sb = lambda name, shape, dt: TB(es.enter_context(nc.sbuf_tensor(uname(name), shape, dt)))
        wb = sb("wb", [128, 24, D], BF16)
        wg = sb("wg", [128, 8, 3 * D], BF16)
        wo = sb("wo", [128, 8, D], BF16)
        gb = sb("gb", [128, D], F32)
        bb = sb("bb", [128, D], F32)
        gbias = sb("gbias", [128, 3, 8], F32)
        hTt = [sb("hTt%d" % i, [128, 8, TG], BF16) for i in range(2)]
        oTt = [[sb("oTt%d_%d" % (r, i), [128, 8, TG], BF16) for i in range(2)] for r in range(3)]
        xin = [sb("xin%d" % i, [128, 2, D], F32) for i in range(2)]
        gtb = [sb("gtb%d" % i, [128, TG], F32) for i in range(2)]
        tmpb = [sb("tmpb%d" % i, [128, TG], F32) for i in range(2)]
        macc = sb("macc", [128, TG], F32)
        mT = sb("mT", [128, 8, TG], BF16)
        st6 = sb("st6", [128, 2, 6], F32)
        mv = sb("mv", [128, 2], F32)
        sd = sb("sd", [128, 1], F32)
        rs = sb("rs", [128, 1], F32)
        nm = sb("nm", [128, 1], F32)
        psA = es.enter_context(nc.psum_tensor(uname("psA"), [128, 4, 512], F32))
        pj = [TB(psA[:, i, :]) for i in range(2)]
        gp = [TB(psA[:, 2 + i, :]) for i in range(2)]
        psY = es.enter_context(nc.psum_tensor(uname("psY"), [128, 2, D], F32))
        py = [TB(psY[:, i, :]) for i in range(2)]
        d_c = k.dsem("c")
        d_i = [k.dsem("i") for _ in range(2)]
        d_o = [k.dsem("o") for _ in range(2)]
        DMAG(k, "sp", [(gb.ap[:], g.partition_broadcast(128)), (bb.ap[:], b.partition_broadcast(128)), (gbias.ap[:], gbias_d)], d_c, [gb, bb, gbias], [])
        stg = [sb("stg%d" % i, [128, 512], F32) for i in range(4)]
        dss = [k.dsem("stg") for _ in range(4)]
        wi = 0
        jobs = []
        for r in range(3):
            for kc in range(8):
                for c0 in (0, 512):
                    jobs.append((wb, r * 8 + kc, c0, w_br[r, kc * 128:(kc + 1) * 128, c0:c0 + 512]))
        for kc in range(8):
            for c0 in range(0, 3 * D, 512):
                jobs.append((wg, kc, c0, w_mix[kc * 128:(kc + 1) * 128, OFF_G + c0:OFF_G + c0 + 512]))
            for c0 in (0, 512):
                jobs.append((wo, kc, c0, w_o[kc * 128:(kc + 1) * 128, c0:c0 + 512]))
        for dst, i0, c0, src in jobs:
            s_ = stg[wi % 4]
            DMAG(k, "sp", [(s_.ap[:], src)], dss[wi % 4], [s_], [])
            en = ("dve", "act", "dve")[wi % 3]
            if en == "act":
                OP(k, en, lambda e, dst=dst, i0=i0, c0=c0, s_=s_: e.copy(out=dst.ap[:, i0, c0:c0 + 512], in_=s_.ap[:]), [dst], [s_])
            else:
                OP(k, en, lambda e, dst=dst, i0=i0, c0=c0, s_=s_: e.tensor_copy(out=dst.ap[:, i0, c0:c0 + 512], in_=s_.ap[:]), [dst], [s_])
            wi += 1
        hv = hT_d.rearrange("(c p) s -> p c s", p=128)
        ovs = [o.rearrange("(c p) s -> p c s", p=128) for o in oT_ds]
        xv_in = x_d.rearrange("(g t p) d -> g p t d", p=128, t=2)
        xv_out = x_out.rearrange("(g t p) d -> g p t d", p=128, t=2)

        def load(gi):
            s = gi % 2
            ts = slice(gi * TG, (gi + 1) * TG)
            DMAG(k, "sp", [(hTt[s].ap[:], hv[:, :, ts])] + [(oTt[r][s].ap[:], ovs[r][:, :, ts]) for r in range(3)] + [(xin[s].ap[:], xv_in[gi])],
                 d_i[s], [hTt[s], oTt[0][s], oTt[1][s], oTt[2][s], xin[s]], [])

        load(0)
        it = 0
        for gi in range(NG):
            s = gi % 2
            if gi + 1 < NG:
                load(gi + 1)
            for dc in range(8):
                for r in range(3):
                    i2 = it % 2
                    it += 1
                    MMG(k, pj[i2], [(pj[i2].ap[:, 0:TG], wb.ap[:, r * 8 + kc, dc * 128:(dc + 1) * 128], oTt[r][s].ap[:, kc, :]) for kc in range(8)], [wb, oTt[r][s]])
                    MMG(k, gp[i2], [(gp[i2].ap[:, 0:TG], wg.ap[:, kc, r * D + dc * 128:r * D + (dc + 1) * 128], hTt[s].ap[:, kc, :]) for kc in range(8)], [wg, hTt[s]])
                    gt, tm = gtb[i2], tmpb[i2]
                    OP(k, "act", lambda e, gt=gt, i2=i2, r=r, dc=dc: e.activation(out=gt.ap[:], in_=gp[i2].ap[:, 0:TG], func=AF.Sigmoid, bias=gbias.ap[:, r, dc:dc + 1], scale=1.0), [gt], [gp[i2], gbias])
                    if r == 0:
                        OP(k, "dve", lambda e, gt=gt, i2=i2: e.tensor_tensor(out=macc.ap[:], in0=pj[i2].ap[:, 0:TG], in1=gt.ap[:], op=ALU.mult), [macc], [pj[i2], gt])
                    else:
                        OP(k, "dve", lambda e, gt=gt, tm=tm, i2=i2: e.tensor_tensor(out=tm.ap[:], in0=pj[i2].ap[:, 0:TG], in1=gt.ap[:], op=ALU.mult), [tm], [pj[i2], gt])
                        if r == 1:
                            OP(k, "pool", lambda e, tm=tm: e.tensor_tensor(out=macc.ap[:], in0=macc.ap[:], in1=tm.ap[:], op=ALU.add), [macc], [tm])
                        else:
                            OP(k, "pool", lambda e, tm=tm, dc=dc: e.tensor_tensor(out=mT.ap[:, dc, :], in0=macc.ap[:], in1=tm.ap[:], op=ALU.add), [mT], [macc, tm])
            for t in range(2):
                for n in range(2):
                    MMG(k, py[t], [(py[t].ap[:, n * 512:(n + 1) * 512], mT.ap[:, dc, t * 128:(t + 1) * 128], wo.ap[:, dc, n * 512:(n + 1) * 512]) for dc in range(8)],
                        [mT, wo], first=(n == 0), start=True)
                xt = xin[s].ap[:, t, :]
                X = xin[s]
                OP(k, "act", lambda e, xt=xt: e.mul(out=xt, in_=xt, mul=ALPHA), [X], [])
                OP(k, "dve", lambda e, xt=xt, t=t: e.tensor_tensor(out=xt, in0=py[t].ap, in1=xt, op=ALU.add), [X], [py[t]])
                ln_tail_tb(k, X, xt, gb, bb, st6, mv, sd, rs, nm)
                DMA(k, "sp", xv_out[gi][:, t, :], xt, d_o[s], [], [X])
        k.barrier()


def ln_tail_tb(k, X, xt, gb, bb, st6, mv, sd, rs, nm):
    for h in range(2):
        OP(k, "dve", lambda e, h=h: e.bn_stats(out=st6.ap[:, h, :], in_=xt[:, h * 512:(h + 1) * 512]), [st6], [X])
    OP(k, "dve", lambda e: e.bn_aggr(out=mv.ap[:], in_=st6.ap[:].rearrange("p a b -> p (a b)")), [mv], [st6])
    OP(k, "act", lambda e: e.activation(out=sd.ap[:], in_=mv.ap[:, 1:2], func=AF.Sqrt, bias=LN_EPS_AP[0][:, 0:1], scale=1.0), [sd], [mv], extra=[LN_EPS_AP[1]])
    OP(k, "dve", lambda e: e.reciprocal(out=rs.ap[:], in_=sd.ap[:]), [rs], [sd])
    OP(k, "dve", lambda e: e.tensor_scalar(out=nm.ap[:], in0=mv.ap[:, 0:1], scalar1=rs.ap[:, 0:1], scalar2=-1.0, op0=ALU.mult, op1=ALU.mult), [nm], [mv, rs])
    OP(k, "act", lambda e: e.activation(out=xt, in_=xt, func=AF.Identity, bias=nm.ap[:, 0:1], scale=rs.ap[:, 0:1]), [X], [nm, rs])
    OP(k, "pool", lambda e: e.tensor_tensor(out=xt, in0=xt, in1=gb.ap[:], op=ALU.mult), [X], [gb])
    OP(k, "pool", lambda e: e.tensor_tensor(out=xt, in0=xt, in1=bb.ap[:], op=ALU.add), [X], [bb])


def build_program(S):
    nc = bass.Bass("TRN2", target_bir_lowering=False)
    L = DEPTH
    din = lambda n, sh, t=F32: nc.dram_tensor(n, sh, t, kind="ExternalInput").ap()
    x = din("x", [S, D])
    ffn_w_in = [din("ffn1_w_in", [L, D, 2 * DFF]), din("ffn2_w_in", [L, D, 2 * DFF])]
    ffn_w_out = [din("ffn1_w_out", [L, DFF, D]), din("ffn2_w_out", [L, DFF, D])]
    ln_g = [din("ln%d_g" % i, [L, D]) for i in (1, 2, 3)]
    ln_b = [din("ln%d_b" % i, [L, D]) for i in (1, 2, 3)]
    w_mix = din("w_mix_in", [L, D, NIN])
    gbias = din("gbias_r", [L, 128, 3, 8])
    lam = din("diff_lambda", [L, 256])
    subg = din("subg_r", [L, 128, 1])
    convw = din("convw_r", [L, 128, 12, 4])
    convb = din("convb_r", [L, 128, 12])
    dtb = din("ssm_dt_bias", [L, 16])
    alog = din("ssm_A_log", [L, 16])
    dsk = din("ssm_D", [L, 16])
    ng = din("ssm_norm_g", [L, D])
    w_br = din("w_branch", [L, 3, D, D])
    w_o = din("w_mix_out", [L, D, D])
    cos = din("c_cos", [128, S])
    sin = din("c_sin", [128, S])
    cst = dict(ident=din("c_ident", [128, 128], BF16), maskA=din("c_maskA", [128, 128], BF16), LT=din("c_LT", [128, 128], BF16),
               mS=din("c_mS", [128, 128], BF16), TRI=din("c_TRI", [128, 128]), SGT=din("c_SGT", [128, 128]), MLE=din("c_MLE", [128, 128]))
    out = nc.dram_tensor("out", [S, D], F32, kind="ExternalOutput").ap()
    xflat = x.bitcast(BF16).rearrange("s c -> (s c)").rearrange("(a b) -> a b", b=S)
    hT_s = xflat[0:D, :]
    w0 = ffn_w_in[0][0].bitcast(BF16)
    oT_s = [xflat[D:2 * D, :], w0[:, 0:S], w0[:, S:2 * S]]
    with ExitStack() as es:
        k = K(nc, es)
        setup_consts(k, es)
        cur = x
        for l in range(L):
            emit_ffn(k, cur, out, ffn_w_in[0][l], ffn_w_out[0][l], ln_g[0][l], ln_b[0][l], cst["ident"], S)
            emit_transpose(k, out, hT_s, cst["ident"], S)
            with ExitStack() as esm:
                hT_res = TB(esm.enter_context(nc.sbuf_tensor(uname("hT_res"), [128, 8, S], BF16)))
                emit_diffattn(k, hT_s, oT_s[0], w_mix[l], lam[l], subg[l], cos, sin, cst, S, l, hT_res=hT_res)
                emit_mamba(k, hT_s, oT_s[1], w_mix[l], convw[l], convb[l], dtb[l], alog[l], dsk[l], ng[l], cst, S, hT_res=hT_res)
                emit_stickbreak(k, hT_s, oT_s[2], w_mix[l], cst, S, hT_res=hT_res)
            emit_merge(k, out, hT_s, oT_s, w_mix[l], gbias[l], w_br[l], w_o[l], ln_g[1][l], ln_b[1][l], out, S)
            emit_ffn(k, out, out, ffn_w_in[1][l], ffn_w_out[1][l], ln_g[2][l], ln_b[2][l], cst["ident"], S)
            cur = out
        print("sem counts", {n: e["cnt"] for n, e in k.engs.items()}, "dma", max(d.cnt for d in k.dsems), flush=True)
    return nc


def host_consts(S):
    import ml_dtypes
    bf = ml_dtypes.bfloat16
    half = 8
    inv = (500000.0 ** (-np.arange(half, dtype=np.float32) * 2.0 / 16)).astype(np.float32)
    ang = (np.arange(S, dtype=np.float32)[:, None] * inv[None, :]).astype(np.float32)
    c = np.ones((128, S), np.float32)
    s = np.zeros((128, S), np.float32)
    for m in range(2):
        for d in range(16):
            c[m * 64 + d] = np.cos(ang[:, d % 8])
            s[m * 64 + d] = np.sin(ang[:, d % 8])
    ii = np.arange(128)
    mA = np.ones((128, 128), np.float32)
    mA[64:, :64] = 0
    return dict(c_cos=c, c_sin=s, c_ident=np.eye(128, dtype=np.float32).astype(bf), c_maskA=mA.astype(bf),
                c_LT=(ii[:, None] >= ii[None, :]).astype(np.float32).astype(bf), c_mS=(ii[:, None] < ii[None, :]).astype(np.float32).astype(bf),
                c_TRI=(ii[:, None] <= ii[None, :]).astype(np.float32), c_SGT=(ii[:, None] > ii[None, :]).astype(np.float32),
                c_MLE=(ii[:, None] <= ii[None, :]).astype(np.float32))


def host_layout(inputs):
    f = lambda a: np.ascontiguousarray(np.asarray(a, dtype=np.float32))
    L = DEPTH
    d = {}
    for n in ("ffn1_w_in", "ffn2_w_in", "ffn1_w_out", "ffn2_w_out", "ln1_g", "ln2_g", "ln3_g", "ln1_b", "ln2_b", "ln3_b", "w_mix_in",
              "ssm_dt_bias", "ssm_A_log", "ssm_D", "ssm_norm_g", "w_branch", "w_mix_out"):
        d[n] = f(inputs[n])
    d["diff_lambda"] = f(inputs["diff_lambda"]).reshape(L, 256)
    d["subg_r"] = f(inputs["diff_subln_g"]).reshape(L, 128, 1)
    d["gbias_r"] = f(np.asarray(inputs["gate_bias"]).reshape(L, 3, 8, 128).transpose(0, 3, 1, 2))
    d["convw_r"] = f(np.asarray(inputs["ssm_conv_w"]).reshape(L, 4, 12, 128).transpose(0, 3, 2, 1))
    d["convb_r"] = f(np.asarray(inputs["ssm_conv_b"]).reshape(L, 12, 128).transpose(0, 2, 1))
    return d


_CACHE = {}


def kernel(**inputs):
    x = np.asarray(inputs["x"], dtype=np.float32)
    B, S, _ = x.shape
    if S not in _CACHE:
        _CACHE[S] = build_program(S)
    nc = _CACHE[S]
    shared = host_layout(inputs)
    shared.update(host_consts(S))
    in_maps = []
    for bi in range(B):
        m = dict(shared)
        m["x"] = np.ascontiguousarray(x[bi])
        in_maps.append(m)
    res = run_bass_kernel_spmd(nc, in_maps, core_ids=list(range(B)))
    return np.stack([np.asarray(r["out"]) for r in res.results], axis=0).astype(np.float32)
```

```python
import math
from contextlib import ExitStack
import numpy as np
import concourse.bass as bass
import concourse.mybir as mybir
from concourse.bass_utils import run_bass_kernel_spmd

F32 = mybir.dt.float32
BF16 = mybir.dt.bfloat16
AF = mybir.ActivationFunctionType
ALU = mybir.AluOpType

D = 1024
DFF = 2816
DEPTH = 2
ALPHA = (2 * DEPTH) ** 0.25
LN_EPS = 1e-5
NIN = 11792


import os
SKIP_SAME_ENGINE = True
_UID = [0]


def uname(n):
    _UID[0] += 1
    return "%s_u%d" % (n, _UID[0])


class Tok:
    __slots__ = ("sem", "val", "key")

    def __init__(self, sem, val, key):
        self.sem, self.val, self.key = sem, val, key


class DSem:
    def __init__(self, sem, key):
        self.sem, self.key, self.cnt = sem, key, 0


class K:
    def __init__(self, nc, es):
        self.nc, self.es = nc, es
        self.engs = {}
        for name, e in (("pe", nc.tensor), ("act", nc.scalar), ("dve", nc.vector),
                        ("pool", nc.gpsimd), ("sp", nc.sync)):
            sem = es.enter_context(nc.semaphore("s_" + name))
            self.engs[name] = {"e": e, "sem": sem, "cnt": 0, "waited": {}, "name": name}
        self.dsems = []
        self.uid = 0

    def begin_phase(self):
        self.pool_i = 0

    def dsem(self, name):
        i = getattr(self, "pool_i", 0)
        self.pool_i = i + 1
        if i < len(self.dsems):
            return self.dsems[i]
        sem = self.es.enter_context(self.nc.semaphore("d_pool_%d" % i))
        d = DSem(sem, "d_pool_%d" % i)
        self.dsems.append(d)
        return d

    def wait(self, engname, deps):
        E = self.engs[engname]
        for t in deps:
            if t is None:
                continue
            if isinstance(t, (list, tuple)):
                self.wait(engname, t)
                continue
            if SKIP_SAME_ENGINE and t.key == "e_pe" and engname == "pe":
                continue
            if E["waited"].get(t.key, 0) < t.val:
                E["e"].wait_ge(t.sem, t.val)
                E["waited"][t.key] = t.val

    def op(self, engname, fn, deps=(), sig=True):
        E = self.engs[engname]
        self.wait(engname, deps)
        inst = fn(E["e"])
        if sig:
            E["cnt"] += 1
            inst.then_inc(E["sem"], 1)
            return Tok(E["sem"], E["cnt"], "e_" + engname)
        return None

    def last(self, engname):
        E = self.engs[engname]
        if E["cnt"] == 0:
            return None
        return Tok(E["sem"], E["cnt"], "e_" + engname)

    def dma(self, q, out, in_, dsem, deps=()):
        E = self.engs[q]
        self.wait(q, deps)
        inst = E["e"].dma_start(out=out, in_=in_)
        dsem.cnt += 16
        inst.then_inc(dsem.sem, 16)
        return Tok(dsem.sem, dsem.cnt, dsem.key)

    def barrier(self):
        toks = [self.last(n) for n in ("pe", "act", "dve", "pool")]
        toks += [Tok(d.sem, d.cnt, d.key) for d in self.dsems if d.cnt > 0]
        for n in ("pe", "act", "dve", "pool", "sp"):
            self.wait(n, toks)


def _mm(out, lhsT, rhs, start, stop):
    return lambda e: e.matmul(out, lhsT, rhs, start=start, stop=stop)


def load_cast_weights(k, es, w_dram, w_sb, nrow_chunks, ncols, colblk, stg_n=4, tag="w"):
    nc = k.nc
    stg = [es.enter_context(nc.sbuf_tensor(uname("stg_%s_%d" % (tag, i)), [128, colblk], F32)) for i in range(stg_n)]
    dss = [k.dsem("stg") for _ in range(stg_n)]
    free = [None] * stg_n
    engs = ("dve", "act", "dve")
    i = 0
    last = []
    for kc in range(nrow_chunks):
        for c0 in range(0, ncols, colblk):
            cw = min(colblk, ncols - c0)
            s = i % stg_n
            t = k.dma("sp", stg[s][:, 0:cw], w_dram[kc * 128:(kc + 1) * 128, c0:c0 + cw], dss[s], deps=[free[s]])
            en = engs[i % 3]
            if en == "act":
                free[s] = k.op(en, lambda e, s=s, kc=kc, c0=c0, cw=cw: e.copy(out=w_sb[:, kc, c0:c0 + cw], in_=stg[s][:, 0:cw]), deps=[t])
            else:
                free[s] = k.op(en, lambda e, s=s, kc=kc, c0=c0, cw=cw: e.tensor_copy(out=w_sb[:, kc, c0:c0 + cw], in_=stg[s][:, 0:cw]), deps=[t])
            i += 1
    return [f for f in free if f is not None]


def emit_ffn(k, x_in, x_out, w_in, w_out, g, b, ident, S):
    nc = k.nc
    k.begin_phase()
    TG = 256
    NG = S // TG
    NJ = DFF // 128
    with ExitStack() as es:
        sb = lambda name, shape, dt: es.enter_context(nc.sbuf_tensor(uname(name), shape, dt))
        w1 = sb("w1", [128, 8, 2 * DFF], BF16)
        w2 = sb("w2", [128, NJ, D], BF16)
        gb = sb("gb", [128, D], F32)
        bb = sb("bb", [128, D], F32)
        idb = sb("idb", [128, 128], BF16)
        xin = [sb("xin%d" % i, [128, 2, D], F32) for i in range(2)]
        xb = sb("xb", [128, 2, D], BF16)
        xT = [sb("xT%d" % i, [128, 8, TG], BF16) for i in range(2)]
        hT = sb("hT", [128, NJ, TG], BF16)
        sg = [sb("sg%d" % i, [128, TG], F32) for i in range(2)]
        st6 = sb("st6", [128, 2, 6], F32)
        mv = sb("mv", [128, 2], F32)
        sd = sb("sd", [128, 1], F32)
        rs = sb("rs", [128, 1], F32)
        nm = sb("nm", [128, 1], F32)
        ps_tp = es.enter_context(nc.psum_tensor(uname("ps_tp"), [128, 2, D], BF16))
        ps_gu = es.enter_context(nc.psum_tensor(uname("ps_gu"), [128, 2, 2, TG], F32))
        ps_y = es.enter_context(nc.psum_tensor(uname("ps_y"), [128, 2, D], F32))
        d_c = k.dsem("const")
        d_x = [k.dsem("xin") for _ in range(2)]
        d_o = [k.dsem("xout") for _ in range(2)]

        tc = [k.dma("sp", gb[:], g.partition_broadcast(128), d_c),
              k.dma("sp", bb[:], b.partition_broadcast(128), d_c),
              k.dma("sp", idb[:], ident, d_c)]
        tc = [tc[-1]]
        wt = load_cast_weights(k, es, w_in, w1, 8, 2 * DFF, 704, tag="w1")
        wt += load_cast_weights(k, es, w_out, w2, NJ, D, 512, tag="w2")

        xv_in = x_in.rearrange("(g t p) d -> g p t d", p=128, t=2)
        xv_out = x_out.rearrange("(g t p) d -> g p t d", p=128, t=2)

        x_tok = [None, None]
        store_tok = [None, None]
        xT_free = [None, None]
        hT_free = None
        gu_free = [None, None]
        sg_free = [None, None]
        y_free = [None, None]
        xb_free = None
        tp_free = [None, None]

        def load(gi):
            s = gi % 2
            x_tok[s] = k.dma("sp", xin[s][:], xv_in[gi], d_x[s], deps=[store_tok[s]])

        load(0)
        for gi in range(NG):
            s = gi % 2
            if gi + 1 < NG:
                load(gi + 1)
            t_xb = k.op("act", lambda e: e.copy(out=xb[:], in_=xin[s][:]), deps=[x_tok[s], xb_free])
            ev = []
            for t in range(2):
                for kc in range(8):
                    tk = k.op("pe", lambda e, t=t, kc=kc: e.transpose(ps_tp[:, t, kc * 128:(kc + 1) * 128], xb[:, t, kc * 128:(kc + 1) * 128], idb[:]),
                              deps=[t_xb, tc, tp_free[t]], sig=(kc == 7))
                tp_free[t] = k.op("dve", lambda e, t=t: e.tensor_copy(out=xT[s][:, :, t * 128:(t + 1) * 128],
                                                                     in_=ps_tp[:, t, :].rearrange("p (c q) -> p c q", q=128)),
                                  deps=[tk, xT_free[s]])
                ev.append(tp_free[t])
            xb_free = tk
            for j in range(NJ):
                bf = j % 2
                for half in range(2):
                    c0 = half * DFF + j * 128
                    for kc in range(8):
                        tk = k.op("pe", _mm(ps_gu[:, bf, half, :], w1[:, kc, c0:c0 + 128], xT[s][:, kc, :], kc == 0, kc == 7),
                                  deps=[ev, wt, gu_free[bf]], sig=(kc == 7 and half == 1))
                t_a = k.op("act", lambda e, bf=bf: e.activation(out=sg[bf][:], in_=ps_gu[:, bf, 0, :], func=AF.Silu),
                           deps=[tk, sg_free[bf]])
                t_d = k.op("dve", lambda e, bf=bf, j=j: e.tensor_tensor(out=hT[:, j, :], in0=sg[bf][:], in1=ps_gu[:, bf, 1, :], op=ALU.mult),
                           deps=[t_a, tk, hT_free])
                gu_free[bf] = t_d
                sg_free[bf] = t_d
            xT_free[s] = tk
            t_h = t_d
            for t in range(2):
                for n in range(2):
                    for j in range(NJ):
                        tk = k.op("pe", _mm(ps_y[:, t, n * 512:(n + 1) * 512], hT[:, j, t * 128:(t + 1) * 128], w2[:, j, n * 512:(n + 1) * 512], j == 0, j == NJ - 1),
                                  deps=[t_h, y_free[t]], sig=(j == NJ - 1 and n == 1))
                xt = xin[s][:, t, :]
                t1 = k.op("act", lambda e, xt=xt: e.mul(out=xt, in_=xt, mul=ALPHA), deps=[xb_free, t_xb])
                t2 = k.op("dve", lambda e, xt=xt, t=t: e.scalar_tensor_tensor(out=xt, in0=ps_y[:, t, :], scalar=0.5, in1=xt, op0=ALU.mult, op1=ALU.add),
                          deps=[t1, tk])
                y_free[t] = t2
                tl = emit_ln_tail(k, xt, gb, bb, st6, mv, sd, rs, nm, t2)
                store_tok[s] = k.dma("sp", xv_out[gi][:, t, :], xt, d_o[s], deps=[tl])
            hT_free = tk
        k.barrier()


def emit_ln_tail(k, xt, gb, bb, st6, mv, sd, rs, nm, dep):
    ta = None
    for h in range(2):
        ta = k.op("dve", lambda e, h=h: e.bn_stats(out=st6[:, h, :], in_=xt[:, h * 512:(h + 1) * 512]), deps=[dep, ta])
    t3 = k.op("dve", lambda e: e.bn_aggr(out=mv[:], in_=st6[:].rearrange("p a b -> p (a b)")), deps=[ta])
    t4 = k.op("act", lambda e: e.activation(out=sd[:], in_=mv[:, 1:2], func=AF.Sqrt, bias=LN_EPS_AP[0][:, 0:1], scale=1.0), deps=[t3, LN_EPS_AP[1]])
    t5 = k.op("dve", lambda e: e.reciprocal(out=rs[:], in_=sd[:]), deps=[t4])
    t6 = k.op("dve", lambda e: e.tensor_scalar(out=nm[:], in0=mv[:, 0:1], scalar1=rs[:, 0:1], scalar2=-1.0, op0=ALU.mult, op1=ALU.mult), deps=[t5])
    t7 = k.op("act", lambda e: e.activation(out=xt, in_=xt, func=AF.Identity, bias=nm[:, 0:1], scale=rs[:, 0:1]), deps=[t6, t5])
    t8 = k.op("pool", lambda e: e.tensor_tensor(out=xt, in0=xt, in1=gb[:], op=ALU.mult), deps=[t7])
    t9 = k.op("pool", lambda e: e.tensor_tensor(out=xt, in0=xt, in1=bb[:], op=ALU.add), deps=[t8])
    return t9


LN_EPS_AP = [None, None]


def setup_consts(k, es):
    nc = k.nc
    eps = es.enter_context(nc.sbuf_tensor("c_eps", [128, 1], F32))
    LN_EPS_AP[1] = k.op("pool", lambda e: e.memset(eps[:], LN_EPS))
    LN_EPS_AP[0] = eps
    one = es.enter_context(nc.sbuf_tensor("c_one", [128, 1], F32))
    ONE_AP[1] = k.op("pool", lambda e: e.memset(one[:], 1.0))
    ONE_AP[0] = one


class TB:
    __slots__ = ("ap", "w", "r")

    def __init__(self, ap):
        self.ap, self.w, self.r = ap, None, {}

    def __getitem__(self, idx):
        return self.ap[idx]


def _deps_of(outs, ins, extra):
    deps = list(extra)
    for o in outs:
        deps += list(o.r.values())
        deps.append(o.w)
    for i in ins:
        deps.append(i.w)
    return deps


def _mark(tok, outs, ins):
    for o in outs:
        o.w = tok
        o.r = {}
    for i in ins:
        if i in outs:
            continue
        old = i.r.get(tok.key)
        if old is None or old.val < tok.val:
            i.r[tok.key] = tok


REC = [None]


class _Proxy:
    def __getattr__(self, name):
        def rec(*a, **kw):
            self.call = (name, a, kw)
            return None
        return rec


def OP(k, eng, fn, outs, ins, extra=()):
    if REC[0] is not None:
        px = _Proxy()
        fn(px)
        name, a, kw = px.call
        REC[0].append(lambda: _OP(k, eng, lambda e: getattr(e, name)(*a, **kw), outs, ins, extra))
        return None
    return _OP(k, eng, fn, outs, ins, extra)


def _OP(k, eng, fn, outs, ins, extra=()):
    tok = k.op(eng, fn, _deps_of(outs, ins, extra))
    _mark(tok, outs, ins)
    return tok


def MMG(k, out, mms, ins, extra=(), first=True, last=True, start=None):
    if REC[0] is not None:
        REC[0].append(lambda: _MMG(k, out, mms, ins, extra, first, last, start))
        return None
    return _MMG(k, out, mms, ins, extra, first, last, start)


def _MMG(k, out, mms, ins, extra=(), first=True, last=True, start=None):
    deps = _deps_of([out] if first else [], ins, extra)
    if not first:
        deps.append(out.w)
    n = len(mms)
    tok = None
    for i, (o, l, r) in enumerate(mms):
        st_ = (first if start is None else start) and i == 0
        tok = k.op("pe", _mm(o, l, r, st_, last and i == n - 1), deps if i == 0 else (), sig=(i == n - 1))
    if first:
        _mark(tok, [out], ins)
    else:
        out.w = tok
        _mark(tok, [], ins)
    return tok


def DMA(k, q, out_ap, in_ap, dsem, outs, ins, extra=()):
    if REC[0] is not None:
        REC[0].append(lambda: _DMA(k, q, out_ap, in_ap, dsem, outs, ins, extra))
        return None
    return _DMA(k, q, out_ap, in_ap, dsem, outs, ins, extra)


def _DMA(k, q, out_ap, in_ap, dsem, outs, ins, extra=()):
    tok = k.dma(q, out_ap, in_ap, dsem, _deps_of(outs, ins, extra))
    _mark(tok, outs, ins)
    return tok


def run_pipeline(jobs, stages, offs=None):
    n, L = len(jobs), len(stages)
    offs = offs or list(range(L))
    for t in range(n + max(offs)):
        for si, st in enumerate(stages):
            j = t - offs[si]
            if 0 <= j < n:
                st(jobs[j])


def DMAG(k, q, pairs, dsem, outs, ins, extra=()):
    deps = _deps_of(outs, ins, extra)
    tok = None
    for i, (o, a) in enumerate(pairs):
        tok = k.dma(q, o, a, dsem, deps if i == 0 else ())
    _mark(tok, outs, ins)
    return tok


SCRATCH_KIND = "ExternalOutput"
OFF_QA, OFF_KA, OFF_VA, OFF_ZB, OFF_XBC, OFF_DT, OFF_QC, OFF_KC, OFF_VC, OFF_G = 0, 1024, 2048, 3072, 4096, 5632, 5648, 6672, 7696, 8720


def emit_diffattn(k, hT_d, oaT_d, w_mix, lam_d, subg_d, cos_d, sin_d, cst, S, layer, stage=9, hT_res=None):
    nc = k.nc
    k.begin_phase()
    NQG = S // 512
    NTB = S // 128
    lam_init = 0.8 - 0.6 * math.exp(-0.3 * layer)
    with ExitStack() as es:
        sb = lambda name, shape, dt: TB(es.enter_context(nc.sbuf_tensor(uname(name), shape, dt)))
        hT = hT_res if hT_res is not None else sb("hT", [128, 8, S], BF16)
        qT = sb("qT", [128, S], BF16)
        kz = [sb("kz%d" % i, [128, S], BF16) for i in range(2)]
        V = sb("V", [128, NTB, 128], BF16)
        cosT = sb("cosT", [128, S], F32)
        sinT = sb("sinT", [128, S], F32)
        wst = sb("wst", [128, 8, 384], F32)
        wA = sb("wA", [128, 8, 384], BF16)
        wrot = sb("wrot", [128, 8, 256], BF16)
        pT = [sb("pT%d" % i, [128, 512], BF16) for i in range(4)]
        t1 = sb("t1", [128, 512], F32)
        t2 = sb("t2", [128, 512], F32)
        e0 = sb("e0", [128, 512], F32)
        e1 = sb("e1", [128, 512], F32)
        e2 = sb("e2", [128, 512], F32)
        sq = sb("sq", [128, 512], BF16)
        oo = [sb("oo%d" % i, [128, 512], BF16) for i in range(2)]
        lamb = sb("lamb", [128, 4, 64], F32)
        lprod = sb("lprod", [128, 2, 64], F32)
        lsum = sb("lsum", [128, 2], F32)
        neglam = sb("neglam", [128, 1], F32)
        gsub = sb("gsub", [128, 1], F32)
        ones = sb("ones", [128, 128], BF16)
        mskA = sb("mskA", [128, 128], BF16)
        psum = es.enter_context(nc.psum_tensor(uname("ps"), [128, 8, 512], F32))
        ps = [TB(psum[:, i, :]) for i in range(8)]
        d_h = k.dsem("hT")
        d_c = k.dsem("c")
        d_w = k.dsem("w")
        d_o = [k.dsem("o") for _ in range(2)]

        hv = hT_d.rearrange("(c p) s -> p c s", p=128)
        if hT_res is None or hT_res.w is None:
            DMAG(k, "sp", [(hT.ap[:, kc, :], hv[:, kc, :]) for kc in range(8)], d_h, [hT], [])
        cp = [(cosT.ap[:], cos_d), (sinT.ap[:], sin_d), (mskA.ap[:], cst["maskA"])]
        cp.append((lamb.ap[:].rearrange("p a d -> p (a d)"), lam_d.partition_broadcast(128)))
        cp.append((gsub.ap[:], subg_d))
        DMAG(k, "sp", cp, d_c, [cosT, sinT, lamb, gsub, mskA], [])
        OP(k, "pool", lambda e: e.memset(ones.ap[:], 1.0), [ones], [])
        OP(k, "pool", lambda e: e.memset(wrot.ap[:], 0.0), [wrot], [])
        OP(k, "pool", lambda e: e.memset(kz[0].ap[:], 0.0), [kz[0]], [])
        OP(k, "pool", lambda e: e.memset(kz[1].ap[:], 0.0), [kz[1]], [])
        lv = lamb.ap[:].rearrange("p (a two) d -> p a two d", two=2)
        OP(k, "dve", lambda e: e.tensor_tensor(out=lprod.ap[:], in0=lv[:, :, 0, :], in1=lv[:, :, 1, :], op=ALU.mult), [lprod], [lamb])
        OP(k, "dve", lambda e: e.tensor_reduce(out=lsum.ap[:], in_=lprod.ap[:], axis=mybir.AxisListType.X, op=ALU.add), [lsum], [lprod])
        OP(k, "act", lambda e: e.activation(out=lsum.ap[:], in_=lsum.ap[:], func=AF.Exp), [lsum], [lsum])
        OP(k, "dve", lambda e: e.tensor_tensor(out=neglam.ap[:], in0=lsum.ap[:, 1:2], in1=lsum.ap[:, 0:1], op=ALU.subtract), [neglam], [lsum])
        OP(k, "dve", lambda e: e.tensor_scalar(out=neglam.ap[:], in0=neglam.ap[:], scalar1=-lam_init, scalar2=None, op0=ALU.add), [neglam], [neglam])
        OP(k, "dve", lambda e: e.tensor_scalar(out=gsub.ap[:], in0=gsub.ap[:], scalar1=1.0 - lam_init, scalar2=None, op0=ALU.mult), [gsub], [gsub])

        pi = 0
        def load_w_A(hA):
            DMAG(k, "sp", [(wst.ap[:, :, j * 128:(j + 1) * 128], w_mix[:, off + hA * 128: off + (hA + 1) * 128].rearrange("(c p) n -> p c n", p=128))
                           for j, off in enumerate((OFF_QA, OFF_KA, OFF_VA))], d_w, [wst], [])
            OP(k, "pool", lambda e: e.tensor_copy(out=wA.ap[:], in_=wst.ap[:]), [wA], [wst])
            for j in range(2):
                for m in range(2):
                    c = j * 128 + m * 64
                    OP(k, "dve", lambda e, c=c: e.tensor_scalar(out=wrot.ap[:, :, c:c + 8], in0=wA.ap[:, :, c + 8:c + 16], scalar1=-1.0, scalar2=None, op0=ALU.mult), [wrot], [wA])
                    OP(k, "dve", lambda e, c=c: e.tensor_copy(out=wrot.ap[:, :, c + 8:c + 16], in_=wA.ap[:, :, c:c + 8]), [wrot], [wA])

        if stage > 0:
            load_w_A(0)
        for hA in range(8 if stage > 0 else 0):
            for g in range(NQG if stage > 1 else 0):
                ts = slice(g * 512, (g + 1) * 512)
                for j in range(2):
                    ba, br = ps[2 * j], ps[2 * j + 1]
                    MMG(k, ba, [(ba.ap, wA.ap[:, kc, j * 128:(j + 1) * 128], hT.ap[:, kc, ts]) for kc in range(8)], [wA, hT])
                    MMG(k, br, [(br.ap, wrot.ap[:, kc, j * 128:(j + 1) * 128], hT.ap[:, kc, ts]) for kc in range(8)], [wrot, hT])
                    OP(k, "dve", lambda e, ba=ba: e.tensor_tensor(out=t1.ap[:], in0=ba.ap, in1=cosT.ap[:, ts], op=ALU.mult), [t1], [ba, cosT])
                    OP(k, "dve", lambda e, br=br: e.tensor_tensor(out=t2.ap[:], in0=br.ap, in1=sinT.ap[:, ts], op=ALU.mult), [t2], [br, sinT])
                    if j == 0:
                        OP(k, "dve", lambda e: e.tensor_tensor(out=qT.ap[:, ts], in0=t1.ap[:], in1=t2.ap[:], op=ALU.add), [qT], [t1, t2])
                    else:
                        OP(k, "dve", lambda e: e.tensor_tensor(out=t1.ap[:], in0=t1.ap[:], in1=t2.ap[:], op=ALU.add), [t1], [t1, t2])
                        for m in range(2):
                            hs_ = slice(m * 64, m * 64 + 64)
                            OP(k, "act", lambda e, m=m, hs_=hs_: e.copy(out=kz[m].ap[hs_, ts], in_=t1.ap[hs_, :]), [kz[m]], [t1])
            for tb in range(NTB if stage > 2 else 0):
                bv = ps[4 + tb % 2]
                MMG(k, bv, [(bv.ap[:, 0:128], hT.ap[:, kc, tb * 128:(tb + 1) * 128], wA.ap[:, kc, 256:384]) for kc in range(8)], [wA, hT])
                OP(k, "act" if tb % 2 else "dve", (lambda e, tb=tb, bv=bv: e.copy(out=V.ap[:, tb, :], in_=bv.ap[:, 0:128])) if tb % 2 else
                   (lambda e, tb=tb, bv=bv: e.tensor_copy(out=V.ap[:, tb, :], in_=bv.ap[:, 0:128])), [V], [bv])
            if hA + 1 < 8:
                load_w_A(hA + 1)
            jobs = []
            for qg in range(NQG if stage > 3 else 0):
                for m in range(2):
                    nkb = 4 * qg + 4
                    for kb in range(nkb):
                        jobs.append(dict(qg=qg, m=m, kb=kb, nkb=nkb, c0=max(0, kb - 4 * qg) * 128, i=len(jobs), last_qg=(m == 1 and kb == nkb - 1)))

            def a1(J):
                qg, m, kb, c0, i = J["qg"], J["m"], J["kb"], J["c0"], J["i"]
                sc, p = ps[i % 3], pT[i % 4]
                MMG(k, sc, [(sc.ap[:, c0:512], kz[m].ap[:, kb * 128:(kb + 1) * 128], qT.ap[:, qg * 512 + c0:(qg + 1) * 512])], [kz[m], qT])
                OP(k, "act", lambda e: e.activation(out=p.ap[:, c0:512], in_=sc.ap[:, c0:512], func=AF.Exp, scale=0.125), [p], [sc])
                if kb >= 4 * qg:
                    OP(k, "pool", lambda e: e.tensor_tensor(out=p.ap[:, c0:c0 + 128], in0=p.ap[:, c0:c0 + 128], in1=mskA.ap[:], op=ALU.mult), [p], [mskA])

            def a2(J):
                qg, m, kb, c0, i, nkb = J["qg"], J["m"], J["kb"], J["c0"], J["i"], J["nkb"]
                p = pT[i % 4]
                ao, asum = ps[4 + m], ps[6 + m]
                MMG(k, ao, [(ao.ap[:, c0:512], V.ap[:, kb, :], p.ap[:, c0:512])], [V, p], first=(kb == 0), last=(kb == nkb - 1))
                MMG(k, asum, [(asum.ap[:, c0:512], ones.ap[:], p.ap[:, c0:512])], [ones, p], first=(kb == 0), last=(kb == nkb - 1))
                if J["last_qg"]:
                    epi(qg)

            def epi(qg):
                OP(k, "dve", lambda e: e.reciprocal(out=e0.ap[:], in_=ps[6].ap), [e0], [ps[6]])
                OP(k, "dve", lambda e: e.tensor_tensor(out=e0.ap[:], in0=ps[4].ap, in1=e0.ap[:], op=ALU.mult), [e0], [ps[4], e0])
                OP(k, "dve", lambda e: e.reciprocal(out=e1.ap[:], in_=ps[7].ap), [e1], [ps[7]])
                OP(k, "dve", lambda e: e.tensor_tensor(out=e1.ap[:], in0=ps[5].ap, in1=e1.ap[:], op=ALU.mult), [e1], [ps[5], e1])
                OP(k, "dve", lambda e: e.scalar_tensor_tensor(out=e2.ap[:], in0=e1.ap[:], scalar=neglam.ap[:, 0:1], in1=e0.ap[:], op0=ALU.mult, op1=ALU.add), [e2], [e0, e1, neglam])
                OP(k, "act", lambda e: e.activation(out=sq.ap[:], in_=e2.ap[:], func=AF.Square), [sq], [e2])
                MMG(k, ps[3], [(ps[3].ap, ones.ap[:], sq.ap[:])], [ones, sq])
                OP(k, "act", lambda e: e.activation(out=e0.ap[:], in_=ps[3].ap, func=AF.Ln, bias=LN_EPS_AP[0][:, 0:1], scale=1.0 / 128), [e0], [ps[3]], extra=[LN_EPS_AP[1]])
                OP(k, "act", lambda e: e.activation(out=e0.ap[:], in_=e0.ap[:], func=AF.Exp, scale=-0.5), [e0], [e0])
                OP(k, "dve", lambda e: e.tensor_tensor(out=e2.ap[:], in0=e2.ap[:], in1=e0.ap[:], op=ALU.mult), [e2], [e0, e2])
                o = oo[qg % 2]
                OP(k, "pool", lambda e, o=o: e.tensor_scalar(out=o.ap[:], in0=e2.ap[:], scalar1=gsub.ap[:, 0:1], scalar2=None, op0=ALU.mult), [o], [e2, gsub])
                DMA(k, "sp", oaT_d[hA * 128:(hA + 1) * 128, qg * 512:(qg + 1) * 512], o.ap[:], d_o[qg % 2], [], [o])

            run_pipeline(jobs, (a1, a2), [0, 2])
        k.barrier()


def emit_stickbreak(k, hT_d, ocT_d, w_mix, cst, S, stage=9, hT_res=None):
    nc = k.nc
    k.begin_phase()
    NQG = S // 512
    NTB = S // 128
    with ExitStack() as es:
        sb = lambda name, shape, dt: TB(es.enter_context(nc.sbuf_tensor(uname(name), shape, dt)))
        hT = hT_res if hT_res is not None else sb("hT", [128, 8, S], BF16)
        qT = sb("qT", [128, S], BF16)
        kz = [sb("kz%d" % i, [128, S], BF16) for i in range(2)]
        Vz = [sb("Vz%d" % i, [128, NTB, 128], BF16) for i in range(2)]
        wst = sb("wst", [128, 8, 384], F32)
        wA = sb("wA", [128, 8, 384], BF16)
        e1b = [sb("e1b%d" % i, [128, 512], F32) for i in range(5)]
        spb = [sb("spb%d" % i, [128, 512], BF16) for i in range(4)]
        t1b = [sb("t1b%d" % i, [128, 512], F32) for i in range(3)]
        ab = [sb("ab%d" % i, [128, 512], BF16) for i in range(3)]
        Rs2 = [sb("Rs%d" % i, [128, 512], F32) for i in range(2)]
        oc = [sb("oc%d" % i, [128, 512], BF16) for i in range(2)]
        ones = sb("ones", [128, 128], BF16)
        LT = sb("LT", [128, 128], BF16)
        mS = sb("mS", [128, 128], BF16)
        zer = sb("zer", [128, 128], BF16)
        psum = es.enter_context(nc.psum_tensor(uname("ps"), [128, 8, 512], F32))
        ps = [TB(psum[:, i, :]) for i in range(8)]
        d_h = k.dsem("hT")
        d_c = k.dsem("c")
        d_w = k.dsem("w")
        d_o = [k.dsem("o") for _ in range(2)]
        hv = hT_d.rearrange("(c p) s -> p c s", p=128)
        if hT_res is None or hT_res.w is None:
            DMAG(k, "sp", [(hT.ap[:, kc, :], hv[:, kc, :]) for kc in range(8)], d_h, [hT], [])
        DMAG(k, "sp", [(LT.ap[:], cst["LT"]), (mS.ap[:], cst["mS"])], d_c, [LT, mS], [])
        OP(k, "pool", lambda e: e.memset(ones.ap[:], 1.0), [ones], [])
        OP(k, "pool", lambda e: e.memset(zer.ap[:], 0.0), [zer], [])
        for i in range(2):
            OP(k, "pool", lambda e, i=i: e.memset(kz[i].ap[:], 0.0), [kz[i]], [])
            OP(k, "pool", lambda e, i=i: e.memset(Vz[i].ap[:], 0.0), [Vz[i]], [])
        it = 0
        def load_w_C(hp):
            DMAG(k, "sp", [(wst.ap[:, :, j * 128:(j + 1) * 128], w_mix[:, off + hp * 128: off + (hp + 1) * 128].rearrange("(c p) n -> p c n", p=128))
                           for j, off in enumerate((OFF_QC, OFF_KC, OFF_VC))], d_w, [wst], [])
            OP(k, "pool", lambda e: e.tensor_copy(out=wA.ap[:], in_=wst.ap[:]), [wA], [wst])

        load_w_C(0)
        for hp in range(8):
            for g in range(NQG):
                ts = slice(g * 512, (g + 1) * 512)
                for j in range(2):
                    ba = ps[j]
                    MMG(k, ba, [(ba.ap, wA.ap[:, kc, j * 128:(j + 1) * 128], hT.ap[:, kc, ts]) for kc in range(8)], [wA, hT])
                    if j == 0:
                        OP(k, "act", lambda e, ba=ba: e.copy(out=qT.ap[:, ts], in_=ba.ap), [qT], [ba])
                    else:
                        for m in range(2):
                            hs_ = slice(m * 64, m * 64 + 64)
                            OP(k, "act", lambda e, ba=ba, m=m, hs_=hs_: e.copy(out=kz[m].ap[hs_, ts], in_=ba.ap[hs_, :]), [kz[m]], [ba])
            for tb in range(NTB if stage > 1 else 0):
                bv = ps[2 + tb % 2]
                MMG(k, bv, [(bv.ap[:, 0:128], hT.ap[:, kc, tb * 128:(tb + 1) * 128], wA.ap[:, kc, 256:384]) for kc in range(8)], [wA, hT])
                if tb % 2 == 0:
                    OP(k, "act", lambda e, tb=tb, bv=bv: e.copy(out=Vz[0].ap[:, tb, 0:64], in_=bv.ap[:, 0:64]), [Vz[0]], [bv])
                    OP(k, "act", lambda e, tb=tb, bv=bv: e.copy(out=Vz[1].ap[:, tb, 64:128], in_=bv.ap[:, 64:128]), [Vz[1]], [bv])
                else:
                    OP(k, "dve", lambda e, tb=tb, bv=bv: e.tensor_copy(out=Vz[0].ap[:, tb, 0:64], in_=bv.ap[:, 0:64]), [Vz[0]], [bv])
                    OP(k, "dve", lambda e, tb=tb, bv=bv: e.tensor_copy(out=Vz[1].ap[:, tb, 64:128], in_=bv.ap[:, 64:128]), [Vz[1]], [bv])
            if hp + 1 < 8:
                load_w_C(hp + 1)
            jobs = []
            for qg in range(NQG if stage > 2 else 0):
                nkb = 4 * qg + 4
                for h2 in range(2):
                    for kb in range(nkb - 1, -1, -1):
                        jobs.append(dict(qg=qg, h2=h2, kb=kb, c0=max(0, kb - 4 * qg) * 128, i=len(jobs),
                                         first_seq=(kb == nkb - 1), first_qg=(h2 == 0 and kb == nkb - 1), last_qg=(h2 == 1 and kb == 0)))

            def s1(J):
                qg, h2, kb, c0, i = J["qg"], J["h2"], J["kb"], J["c0"], J["i"]
                cs = slice(c0, 512)
                z, e1 = ps[i % 3], e1b[i % 5]
                OP(k, "act", lambda e: e.activation(out=e1.ap[:, cs], in_=z.ap[:, cs], func=AF.Exp, scale=0.125), [e1], [z])

            def s0(J):
                qg, h2, kb, c0, i = J["qg"], J["h2"], J["kb"], J["c0"], J["i"]
                cs = slice(c0, 512)
                z = ps[i % 3]
                MMG(k, z, [(z.ap[:, cs], kz[h2].ap[:, kb * 128:(kb + 1) * 128], qT.ap[:, qg * 512 + c0:(qg + 1) * 512])], [kz[h2], qT])

            def s1b(J):
                qg, kb, c0, i = J["qg"], J["kb"], J["c0"], J["i"]
                cs = slice(c0, 512)
                e1, sp = e1b[i % 5], spb[i % 4]
                OP(k, "act", lambda e: e.activation(out=sp.ap[:, cs], in_=e1.ap[:, cs], func=AF.Ln, bias=1.0, scale=1.0), [sp], [e1], extra=[ONE_AP[1]])
                if kb >= 4 * qg:
                    OP(k, "pool", lambda e: e.tensor_tensor(out=sp.ap[:, c0:c0 + 128], in0=sp.ap[:, c0:c0 + 128], in1=mS.ap[:], op=ALU.mult), [sp], [mS])
                    OP(k, "pool", lambda e: e.tensor_tensor(out=e1.ap[:, c0:c0 + 128], in0=e1.ap[:, c0:c0 + 128], in1=mS.ap[:], op=ALU.mult), [e1], [mS])

            def s2(J):
                c0, i = J["c0"], J["i"]
                cs = slice(c0, 512)
                sp, t1 = spb[i % 4], t1b[i % 3]
                Cb, Sb = ps[3 + i % 2], ps[5 + i % 2]
                Rc, Rn = Rs2[i % 2], Rs2[(i + 1) % 2]
                if J["first_seq"]:
                    OP(k, "pool", lambda e: e.memset(Rc.ap[:], 0.0), [Rc], [])
                    OP(k, "pool", lambda e: e.memset(Rn.ap[:], 0.0), [Rn], [])
                MMG(k, Cb, [(Cb.ap[:, cs], LT.ap[:], sp.ap[:, cs])], [LT, sp])
                MMG(k, Sb, [(Sb.ap[:, cs], ones.ap[:], sp.ap[:, cs])], [ones, sp])
                OP(k, "dve", lambda e: e.tensor_tensor(out=t1.ap[:, cs], in0=Cb.ap[:, cs], in1=Rc.ap[:, cs], op=ALU.add), [t1], [Cb, Rc])
                OP(k, "dve", lambda e: e.tensor_tensor(out=Rn.ap[:, cs], in0=Sb.ap[:, cs], in1=Rc.ap[:, cs], op=ALU.add), [Rn], [Sb, Rc])

            def s2b(J):
                c0, i = J["c0"], J["i"]
                cs = slice(c0, 512)
                e1, t1, a = e1b[i % 5], t1b[i % 3], ab[i % 3]
                OP(k, "act", lambda e: e.activation(out=t1.ap[:, cs], in_=t1.ap[:, cs], func=AF.Exp, scale=-1.0), [t1], [t1])
                OP(k, "dve" if i % 4 == 3 else "pool", lambda e: e.tensor_tensor(out=a.ap[:, cs], in0=e1.ap[:, cs], in1=t1.ap[:, cs], op=ALU.mult), [a], [e1, t1])

            def s3(J):
                qg, h2, kb, c0, i = J["qg"], J["h2"], J["kb"], J["c0"], J["i"]
                cs = slice(c0, 512)
                a = ab[i % 3]
                acc = ps[7]
                if J["first_qg"]:
                    MMG(k, acc, [(acc.ap, zer.ap[:], qT.ap[:, 0:512])], [zer, qT], first=True, last=False)
                MMG(k, acc, [(acc.ap[:, cs], Vz[h2].ap[:, kb, :], a.ap[:, cs])], [Vz[h2], a], first=False, last=J["last_qg"])
                if J["last_qg"]:
                    o = oc[qg % 2]
                    OP(k, "act", lambda e: e.copy(out=o.ap[:], in_=acc.ap), [o], [acc])
                    DMA(k, "sp", ocT_d[hp * 128:(hp + 1) * 128, qg * 512:(qg + 1) * 512], o.ap[:], d_o[qg % 2], [], [o])

            run_pipeline(jobs, (s0, s1, s2, s2b, s1b, s3), [0, 1, 3, 4, 1, 5])
        k.barrier()


ONE_AP = [None, None]


def emit_mamba(k, hT_d, obT_d, w_mix, convw_d, convb_d, dtb_d, alog_d, dsk_d, ng_d, cst, S, hT_res=None):
    nc = k.nc
    k.begin_phase()
    NG = S // 512
    with ExitStack() as es:
        sb = lambda name, shape, dt: TB(es.enter_context(nc.sbuf_tensor(uname(name), shape, dt)))
        hT = hT_res if hT_res is not None else sb("hT", [128, 8, S], BF16)
        wz = sb("wz", [128, 8, 1024], BF16)
        wx = sb("wx", [128, 8, 1536], BF16)
        wdt = sb("wdt", [128, 8, 16], BF16)
        wst = [sb("wst%d" % i, [128, 8, 128], F32) for i in range(2)]
        wst16 = sb("wst16", [128, 8, 16], F32)
        hist = sb("hist", [128, 12, 3], F32)
        rw = [sb("rw%d" % i, [128, 515], F32) for i in range(2)]
        cacc = [sb("cacc%d" % i, [128, 512], F32) for i in range(2)]
        xbcT = sb("xbcT", [128, 12, 512], BF16)
        xs_tok = sb("xs_tok", [128, 4, 1024], BF16)
        B_tok = sb("B_tok", [128, 4, 256], BF16)
        dtp = sb("dtp", [128, 16], F32)
        dt = sb("dt", [128, 16], F32)
        dtA = sb("dtA", [128, 16], F32)
        ex2 = [sb("ex%d" % i, [128, 48], F32) for i in range(2)]
        xdt = sb("xdt", [128, 16, 64], BF16)
        xdte2 = [sb("xdte%d" % i, [128, 16, 64], BF16) for i in range(2)]
        CI = [0]
        PT = [None]
        cbs = sb("cbs", [128, 2, 128], F32)
        cbm = sb("cbm", [128, 2, 128], F32)
        A4 = [sb("A4%d" % i, [128, 4, 128], F32) for i in range(2)]
        dec = [sb("dec%d" % i, [128, 4, 128], F32) for i in range(2)]
        M4 = [sb("M4%d" % i, [128, 4, 128], BF16) for i in range(2)]
        zs2 = [[sb("zs%d_%d" % (j, i), [128, 512], F32) for i in range(2)] for j in range(2)]
        yd2 = [[sb("yd%d_%d" % (j, i), [128, 512], F32) for i in range(2)] for j in range(2)]
        yg = [sb("yg%d" % i, [128, 512], F32) for i in range(2)]
        dx = sb("dx", [128, 512], F32)
        junk = sb("junk", [128, 512], F32)
        ssq = sb("ssq", [128, 2], F32)
        rs2 = sb("rs2", [128, 2], F32)
        ob = sb("ob", [128, 1024], BF16)
        obT = [sb("obT%d" % i, [128, 8, 128], BF16) for i in range(2)]
        H = [sb("H%d" % i, [128, 512], F32) for i in range(2)]
        Hbf = [sb("Hbf%d" % i, [128, 512], BF16) for i in range(2)]
        TRI = sb("TRI", [128, 128], F32)
        SGT = sb("SGT", [128, 128], F32)
        MLE = sb("MLE", [128, 128], F32)
        ones32 = sb("ones32", [128, 128], F32)
        idb = sb("idb", [128, 128], BF16)
        convw = sb("convw", [128, 12, 4], F32)
        convb = sb("convb", [128, 12], F32)
        dtbias = sb("dtbias", [128, 16], F32)
        aneg = sb("aneg", [128, 16], F32)
        dskip = sb("dskip", [128, 16], F32)
        normg = sb("normg", [128, 1024], F32)
        psum = es.enter_context(nc.psum_tensor(uname("ps"), [128, 7, 512], F32))
        ps = [TB(psum[:, i, :]) for i in range(7)]
        psT = TB(es.enter_context(nc.psum_tensor(uname("psT"), [128, 1024], BF16)))
        d_h, d_c, d_o = k.dsem("hT"), k.dsem("c"), [k.dsem("o") for _ in range(2)]
        d_w = [k.dsem("w") for _ in range(2)]
        d_w16 = k.dsem("w16")

        hv = hT_d.rearrange("(c p) s -> p c s", p=128)
        if hT_res is None or hT_res.w is None:
            DMAG(k, "sp", [(hT.ap[:, kc, :], hv[:, kc, :]) for kc in range(8)], d_h, [hT], [])
        DMAG(k, "sp", [(TRI.ap[:], cst["TRI"]), (SGT.ap[:], cst["SGT"]), (MLE.ap[:], cst["MLE"]), (idb.ap[:], cst["ident"]),
                       (convw.ap[:], convw_d), (convb.ap[:], convb_d), (dtbias.ap[:], dtb_d.partition_broadcast(128)),
                       (aneg.ap[:], alog_d.partition_broadcast(128)), (dskip.ap[:], dsk_d.partition_broadcast(128)),
                       (normg.ap[:], ng_d.partition_broadcast(128))], d_c,
             [TRI, SGT, MLE, idb, convw, convb, dtbias, aneg, dskip, normg], [])
        OP(k, "pool", lambda e: e.memset(ones32.ap[:], 1.0), [ones32], [])
        OP(k, "pool", lambda e: e.memset(hist.ap[:], 0.0), [hist], [])
        for i in range(2):
            OP(k, "pool", lambda e, i=i: e.memset(H[i].ap[:], 0.0), [H[i]], [])
            OP(k, "pool", lambda e, i=i: e.memset(Hbf[i].ap[:], 0.0), [Hbf[i]], [])
        OP(k, "act", lambda e: e.activation(out=aneg.ap[:], in_=aneg.ap[:], func=AF.Exp), [aneg], [aneg])
        OP(k, "dve", lambda e: e.tensor_scalar(out=aneg.ap[:], in0=aneg.ap[:], scalar1=-1.0, scalar2=None, op0=ALU.mult), [aneg], [aneg])
        wi = 0
        for dst, off, n in ((wz, OFF_ZB, 1024), (wx, OFF_XBC, 1536)):
            for c0 in range(0, n, 128):
                s_ = wst[wi % 2]
                DMAG(k, "sp", [(s_.ap[:], w_mix[:, off + c0: off + c0 + 128].rearrange("(c p) n -> p c n", p=128))], d_w[wi % 2], [s_], [])
                OP(k, "act" if wi % 2 else "dve", lambda e, dst=dst, c0=c0, s_=s_: (e.copy if wi % 2 else e.tensor_copy)(out=dst.ap[:, :, c0:c0 + 128], in_=s_.ap[:]), [dst], [s_])
                wi += 1
        DMAG(k, "sp", [(wst16.ap[:], w_mix[:, OFF_DT:OFF_DT + 16].rearrange("(c p) n -> p c n", p=128))], d_w16, [wst16], [])
        OP(k, "dve", lambda e: e.tensor_copy(out=wdt.ap[:], in_=wst16.ap[:]), [wdt], [wst16])

        for g in range(NG):
            gs = slice(g * 512, (g + 1) * 512)
            for cc in range(12):
                bk = ps[cc % 2]
                r_, ca = rw[cc % 2], cacc[cc % 2]
                MMG(k, bk, [(bk.ap, wx.ap[:, kc, cc * 128:(cc + 1) * 128], hT.ap[:, kc, gs]) for kc in range(8)], [wx, hT])
                OP(k, "pool", lambda e, r_=r_, cc=cc: e.tensor_copy(out=r_.ap[:, 0:3], in_=hist.ap[:, cc, :]), [r_], [hist])
                OP(k, "act", lambda e, r_=r_, bk=bk: e.copy(out=r_.ap[:, 3:515], in_=bk.ap), [r_], [bk])
                OP(k, "pool", lambda e, r_=r_, cc=cc: e.tensor_copy(out=hist.ap[:, cc, :], in_=r_.ap[:, 512:515]), [hist], [r_])
                OP(k, "dve", lambda e, r_=r_, ca=ca, cc=cc: e.tensor_scalar(out=ca.ap[:], in0=r_.ap[:, 0:512], scalar1=convw.ap[:, cc, 0:1], scalar2=None, op0=ALU.mult), [ca], [r_, convw])
                for j in range(1, 4):
                    OP(k, "dve", lambda e, r_=r_, ca=ca, cc=cc, j=j: e.scalar_tensor_tensor(out=ca.ap[:], in0=r_.ap[:, j:j + 512], scalar=convw.ap[:, cc, j:j + 1], in1=ca.ap[:], op0=ALU.mult, op1=ALU.add), [ca], [r_, convw])
                OP(k, "act", lambda e, ca=ca, cc=cc: e.activation(out=xbcT.ap[:, cc, :], in_=ca.ap[:], func=AF.Silu, bias=convb.ap[:, cc:cc + 1], scale=1.0), [xbcT], [ca, convb])
            for tb4 in range(4):
                ls = slice(tb4 * 128, (tb4 + 1) * 128)
                MMT = []
                deps = _deps_of([psT], [xbcT, idb], [])
                for cc in range(8):
                    tk = k.op("pe", lambda e, cc=cc: e.transpose(psT.ap[:, cc * 128:(cc + 1) * 128], xbcT.ap[:, cc, ls], idb.ap[:]), deps if cc == 0 else (), sig=(cc == 7))
                _mark(tk, [psT], [xbcT, idb])
                OP(k, "act", lambda e, tb4=tb4: e.copy(out=xs_tok.ap[:, tb4, :], in_=psT.ap[:]), [xs_tok], [psT])
                deps = _deps_of([psT], [xbcT, idb], [])
                for cc in range(2):
                    tk = k.op("pe", lambda e, cc=cc: e.transpose(psT.ap[:, cc * 128:(cc + 1) * 128], xbcT.ap[:, 8 + cc, ls], idb.ap[:]), deps if cc == 0 else (), sig=(cc == 1))
                _mark(tk, [psT], [xbcT, idb])
                OP(k, "act", lambda e, tb4=tb4: e.copy(out=B_tok.ap[:, tb4, :], in_=psT.ap[:, 0:256]), [B_tok], [psT])
            def front(tb4):
                c = g * 4 + tb4
                p = c % 2
                T = slice(c * 128, (c + 1) * 128)
                ls = slice(tb4 * 128, (tb4 + 1) * 128)
                sm = ps[6]
                ex = ex2[p]
                xdte = xdte2[p]
                MMG(k, sm, [(sm.ap[:, 0:16], hT.ap[:, kc, T], wdt.ap[:, kc, :]) for kc in range(8)], [hT, wdt])
                OP(k, "dve", lambda e: e.tensor_tensor(out=dtp.ap[:], in0=sm.ap[:, 0:16], in1=dtbias.ap[:], op=ALU.add), [dtp], [sm, dtbias])
                OP(k, "act", lambda e: e.activation(out=dtp.ap[:], in_=dtp.ap[:], func=AF.Exp), [dtp], [dtp])
                OP(k, "act", lambda e: e.activation(out=dt.ap[:], in_=dtp.ap[:], func=AF.Ln, bias=1.0, scale=1.0), [dt], [dtp], extra=[ONE_AP[1]])
                OP(k, "dve", lambda e: e.tensor_tensor(out=dtA.ap[:], in0=dt.ap[:], in1=aneg.ap[:], op=ALU.mult), [dtA], [dt, aneg])
                for i, m_ in enumerate((TRI, SGT, ones32)):
                    MMG(k, sm, [(sm.ap[:, 16 + i * 16:32 + i * 16], m_.ap[:], dtA.ap[:])], [m_, dtA], first=False, start=True)
                OP(k, "act", lambda e: e.activation(out=ex.ap[:], in_=sm.ap[:, 16:64], func=AF.Exp), [ex], [sm])
                xs3 = xs_tok.ap[:, tb4, :].rearrange("p (h d) -> p h d", d=64)
                OP(k, "pool", lambda e: e.tensor_tensor(out=xdt.ap[:], in0=xs3, in1=dt.ap[:].unsqueeze(2).to_broadcast([128, 16, 64]), op=ALU.mult), [xdt], [xs_tok, dt])
                OP(k, "pool", lambda e: e.tensor_tensor(out=xdte.ap[:], in0=xdt.ap[:], in1=ex.ap[:, 16:32].unsqueeze(2).to_broadcast([128, 16, 64]), op=ALU.mult), [xdte], [xdt, ex])
                for gg in range(2):
                    MMG(k, sm, [(sm.ap[:, 64 + gg * 128:192 + gg * 128], xbcT.ap[:, 8 + gg, ls], xbcT.ap[:, 10 + gg, ls])], [xbcT], first=(gg == 0), start=True)
                OP(k, "act", lambda e: e.copy(out=cbs.ap[:].rearrange("p a b -> p (a b)"), in_=sm.ap[:, 64:320]), [cbs], [sm])
                OP(k, "pool", lambda e: e.tensor_tensor(out=cbm.ap[:], in0=cbs.ap[:], in1=MLE.ap[:].unsqueeze(1).to_broadcast([128, 2, 128]), op=ALU.mult), [cbm], [cbs, MLE])
                for gg in range(2):
                    G = slice(gg * 512, (gg + 1) * 512)
                    z_ = zs2[p][gg]
                    MMG(k, ps[0], [(ps[0].ap, hT.ap[:, kc, T], wz.ap[:, kc, G]) for kc in range(8)], [hT, wz])
                    OP(k, "act", lambda e: e.activation(out=z_.ap[:], in_=ps[0].ap, func=AF.Silu), [z_], [ps[0]])
                    for r in range(2):
                        i2 = CI[0] % 2
                        CI[0] += 1
                        a4, dc, m4, sg_ = A4[i2], dec[i2], M4[i2], ps[4 + i2]
                        for i in range(4):
                            h = gg * 8 + r * 4 + i
                            OP(k, "dve", lambda e, i=i, h=h: e.tensor_scalar(out=a4.ap[:, i, :], in0=SGT.ap[:], scalar1=dtA.ap[:, h:h + 1], scalar2=None, op0=ALU.mult), [a4], [SGT, dtA])
                        for i in range(4):
                            MMG(k, sg_, [(sg_.ap[:, i * 128:(i + 1) * 128], a4.ap[:, i, :], TRI.ap[:])], [a4, TRI], first=(i == 0), start=True)
                        OP(k, "act", lambda e: e.activation(out=dc.ap[:].rearrange("p a b -> p (a b)"), in_=sg_.ap, func=AF.Exp), [dc], [sg_])
                        OP(k, "pool", lambda e: e.tensor_tensor(out=m4.ap[:], in0=dc.ap[:], in1=cbm.ap[:, gg, :].unsqueeze(1).to_broadcast([128, 4, 128]), op=ALU.mult), [m4], [dc, cbm])
                        for i in range(4):
                            h = gg * 8 + r * 4 + i
                            hh = r * 4 + i
                            MMG(k, ps[2], [(ps[2].ap[:, hh * 64:(hh + 1) * 64], m4.ap[:, i, :], xdt.ap[:, h, :])], [m4, xdt], first=(r == 0 and i == 0), start=True)
                    yd_ = yd2[p][gg]
                    OP(k, "act", lambda e: e.copy(out=yd_.ap[:], in_=ps[2].ap), [yd_], [ps[2]])

            def tail(tb4):
                c = g * 4 + tb4
                p = c % 2
                T = slice(c * 128, (c + 1) * 128)
                ls = slice(tb4 * 128, (tb4 + 1) * 128)
                ex = ex2[p]
                xdte = xdte2[p]
                for gg in range(2):
                    MMG(k, ps[1], [(ps[1].ap, xbcT.ap[:, 10 + gg, ls], Hbf[gg].ap[:])], [xbcT, Hbf[gg]])
                    y_ = yg[gg]
                    y3 = y_.ap[:].rearrange("p (h d) -> p h d", d=64)
                    OP(k, "dve", lambda e: e.tensor_tensor(out=y3, in0=ps[1].ap.rearrange("p (h d) -> p h d", d=64),
                                                           in1=ex.ap[:, gg * 8:(gg + 1) * 8].unsqueeze(2).to_broadcast([128, 8, 64]), op=ALU.mult), [y_], [ps[1], ex])
                    OP(k, "dve", lambda e: e.tensor_tensor(out=y_.ap[:], in0=y_.ap[:], in1=yd2[p][gg].ap[:], op=ALU.add), [y_], [yd2[p][gg]])
                    OP(k, "pool", lambda e: e.tensor_tensor(out=dx.ap[:].rearrange("p (h d) -> p h d", d=64), in0=xs_tok.ap[:, tb4, gg * 512:(gg + 1) * 512].rearrange("p (h d) -> p h d", d=64),
                                                            in1=dskip.ap[:, gg * 8:(gg + 1) * 8].unsqueeze(2).to_broadcast([128, 8, 64]), op=ALU.mult), [dx], [xs_tok, dskip])
                    OP(k, "pool", lambda e: e.tensor_tensor(out=y_.ap[:], in0=y_.ap[:], in1=dx.ap[:], op=ALU.add), [y_], [dx])
                    OP(k, "dve", lambda e: e.tensor_tensor(out=y_.ap[:], in0=y_.ap[:], in1=zs2[p][gg].ap[:], op=ALU.mult), [y_], [zs2[p][gg]])
                    OP(k, "act", lambda e: e.activation(out=junk.ap[:], in_=y_.ap[:], func=AF.Square, accum_out=ssq.ap[:, gg:gg + 1]), [junk, ssq], [y_])
                    MMG(k, ps[3], [(ps[3].ap, B_tok.ap[:, tb4, gg * 128:(gg + 1) * 128], xdte.ap[:, gg * 8:(gg + 1) * 8, :].rearrange("p h d -> p (h d)"))], [B_tok, xdte])
                    H3 = H[gg].ap[:].rearrange("p (h d) -> p h d", d=64)
                    OP(k, "dve", lambda e: e.tensor_tensor(out=H3, in0=H3, in1=ex.ap[:, 32 + gg * 8:40 + gg * 8].unsqueeze(2).to_broadcast([128, 8, 64]), op=ALU.mult), [H[gg]], [ex])
                    OP(k, "dve", lambda e: e.tensor_tensor(out=H[gg].ap[:], in0=ps[3].ap, in1=H[gg].ap[:], op=ALU.add), [H[gg]], [ps[3]])
                    OP(k, "act", lambda e: e.copy(out=Hbf[gg].ap[:], in_=H[gg].ap[:]), [Hbf[gg]], [H[gg]])
                OP(k, "dve", lambda e: e.tensor_scalar(out=rs2.ap[:], in0=ssq.ap[:], scalar1=1.0 / 512, scalar2=LN_EPS, op0=ALU.mult, op1=ALU.add), [rs2], [ssq])
                OP(k, "act", lambda e: e.activation(out=rs2.ap[:], in_=rs2.ap[:], func=AF.Ln), [rs2], [rs2])
                OP(k, "act", lambda e: e.activation(out=rs2.ap[:], in_=rs2.ap[:], func=AF.Exp, scale=-0.5), [rs2], [rs2])
                for gg in range(2):
                    G = slice(gg * 512, (gg + 1) * 512)
                    OP(k, "dve", lambda e, gg=gg, G=G: e.scalar_tensor_tensor(out=ob.ap[:, G], in0=yg[gg].ap[:], scalar=rs2.ap[:, gg:gg + 1], in1=normg.ap[:, G], op0=ALU.mult, op1=ALU.mult), [ob], [yg[gg], rs2, normg])
                def _tr():
                    deps = _deps_of([psT], [ob, idb], [])
                    for cc in range(8):
                        tk = k.op("pe", lambda e, cc=cc: e.transpose(psT.ap[:, cc * 128:(cc + 1) * 128], ob.ap[:, cc * 128:(cc + 1) * 128], idb.ap[:]), deps if cc == 0 else (), sig=(cc == 7))
                    _mark(tk, [psT], [ob, idb])
                if REC[0] is not None:
                    REC[0].append(_tr)
                else:
                    _tr()
                o_ = obT[c % 2]
                OP(k, "act", lambda e: e.copy(out=o_.ap[:].rearrange("p a b -> p (a b)"), in_=psT.ap[:]), [o_], [psT])
                DMA(k, "sp", obT_d.rearrange("(c p) s -> p c s", p=128)[:, :, T], o_.ap[:], d_o[c % 2], [], [o_])

            def record(fn, arg):
                REC[0] = []
                fn(arg)
                r, REC[0] = REC[0], None
                return r

            prev_tail = None
            for tb4 in range(4):
                fr = record(front, tb4)
                tl = prev_tail or []
                nf, nt = len(fr), len(tl)
                fi = ti = 0
                while fi < nf or ti < nt:
                    if fi < nf and (ti >= nt or fi * nt <= ti * nf):
                        fr[fi]()
                        fi += 1
                    else:
                        tl[ti]()
                        ti += 1
                prev_tail = record(tail, tb4)
            for th in prev_tail:
                th()
        k.barrier()


def emit_transpose(k, x_d, hT_d, ident, S):
    nc = k.nc
    k.begin_phase()
    with ExitStack() as es:
        sb = lambda name, shape, dt: TB(es.enter_context(nc.sbuf_tensor(uname(name), shape, dt)))
        xin = [sb("xin%d" % i, [128, D], F32) for i in range(2)]
        xb = [sb("xb%d" % i, [128, D], BF16) for i in range(2)]
        xo = [sb("xo%d" % i, [128, 8, 128], BF16) for i in range(2)]
        idb = sb("idb", [128, 128], BF16)
        psT = [TB(es.enter_context(nc.psum_tensor(uname("psT"), [128, 1024], BF16))) for _ in range(2)]
        d_c = k.dsem("c")
        d_x = [k.dsem("x") for _ in range(2)]
        d_o = [k.dsem("o") for _ in range(2)]
        DMAG(k, "sp", [(idb.ap[:], ident)], d_c, [idb], [])
        hv = hT_d.rearrange("(c p) s -> p c s", p=128)
        for t in range(S // 128):
            s = t % 2
            DMAG(k, "sp", [(xin[s].ap[:], x_d[t * 128:(t + 1) * 128, :])], d_x[s], [xin[s]], [])
            OP(k, "act", lambda e, s=s: e.copy(out=xb[s].ap[:], in_=xin[s].ap[:]), [xb[s]], [xin[s]])
            deps = _deps_of([psT[s]], [xb[s], idb], [])
            for cc in range(8):
                tk = k.op("pe", lambda e, cc=cc, s=s: e.transpose(psT[s].ap[:, cc * 128:(cc + 1) * 128], xb[s].ap[:, cc * 128:(cc + 1) * 128], idb.ap[:]), deps if cc == 0 else (), sig=(cc == 7))
            _mark(tk, [psT[s]], [xb[s], idb])
            OP(k, "dve", lambda e, s=s: e.tensor_copy(out=xo[s].ap[:].rearrange("p a b -> p (a b)"), in_=psT[s].ap[:]), [xo[s]], [psT[s]])
            DMA(k, "sp", hv[:, :, t * 128:(t + 1) * 128], xo[s].ap[:], d_o[s], [], [xo[s]])
        k.barrier()


def emit_merge(k, x_d, hT_d, oT_ds, w_mix, gbias_d, w_br, w_o, g, b, x_out, S):
    nc = k.nc
    k.begin_phase()
    TG = 256
    NG = S // TG
    with ExitStack() as es:
        sb = lambda name, shape, dt: TB(es.enter_context(nc.sbuf_tensor(uname(name), shape, dt)))
        wb = sb("wb", [128, 24, D], BF16)
        wg = sb("wg", [128, 8, 3 * D], BF16)
        wo = sb("wo", [128, 8, D], BF16)
        gb = sb("gb", [128, D], F32)
        bb = sb("bb", [128, D], F32)
        gbias = sb("gbias", [128, 3, 8], F32)
        hTt = [sb("hTt%d" % i, [128, 8, TG], BF16) for i in range(2)]
        oTt = [[sb("oTt%d_%d" % (r, i), [128, 8, TG], BF16) for i in range(2)] for r in range(3)]
        xin = [sb("xin%d" % i, [128, 2, D], F32) for i in range(2)]
        gtb = [sb("gtb%d" % i, [128, TG], F32) for i in range(2)]
        tmpb = [sb("tmpb%d" % i, [128, TG], F32) for i in range(2)]
        macc = sb("macc", [128, TG], F32)
        mT = sb("mT", [128, 8, TG], BF16)
        st6 = sb("st6", [128, 2, 6], F32)
        mv = sb("mv", [128, 2], F32)
        sd = sb("sd", [128, 1], F32)
        rs = sb("rs", [128, 1], F32)
        nm = sb("nm", [128, 1], F32)
        psA = es.enter_context(nc.psum_tensor(uname("psA"), [128, 4, 512], F32))
        pj = [TB(psA[:, i, :]) for i in range(2)]
        gp = [TB(psA[:, 2 + i, :]) for i in range(2)]
        psY = es.enter_context(nc.psum_tensor(uname("psY"), [128, 2, D], F32))
        py = [TB(psY[:, i, :]) for i in range(2)]
        d_c = k.dsem("c")
        d_i = [k.dsem("i") for _ in range(2)]
        d_o = [k.dsem("o") for _ in range(2)]
        DMAG(k, "sp", [(gb.ap[:], g.partition_broadcast(128)), (bb.ap[:], b.partition_broadcast(128)), (gbias.ap[:], gbias_d)], d_c, [gb, bb, gbias], [])
        stg = [sb("stg%d" % i, [128, 512], F32) for i in range(4)]
        dss = [k.dsem("stg") for _ in range(4)]
        wi = 0
        jobs = []
        for r in range(3):
            for kc in range(8):
                for c0 in (0, 512):
                    jobs.append((wb, r * 8 + kc, c0, w_br[r, kc * 128:(kc + 1) * 128, c0:c0 + 512]))
        for kc in range(8):
            for c0 in range(0, 3 * D, 512):
                jobs.append((wg, kc, c0, w_mix[kc * 128:(kc + 1) * 128, OFF_G + c0:OFF_G + c0 + 512]))
            for c0 in (0, 512):
                jobs.append((wo, kc, c0, w_o[kc * 128:(kc + 1) * 128, c0:c0 + 512]))
        for dst, i0, c0, src in jobs:
            s_ = stg[wi % 4]
            DMAG(k, "sp", [(s_.ap[:], src)], dss[wi % 4], [s_], [])
            en = ("dve", "act", "dve")[wi % 3]
            if en == "act":
                OP(k, en, lambda e, dst=dst, i0=i0, c0=c0, s_=s_: e.copy(out=dst.ap[:, i0, c0:c0 + 512], in_=s_.ap[:]), [dst], [s_])
            else:
                OP(k, en, lambda e, dst=dst, i0=i0, c0=c0, s_=s_: e.tensor_copy(out=dst.ap[:, i0, c0:c0 + 512], in_=s_.ap[:]), [dst], [s_])
            wi += 1
        hv = hT_d.rearrange("(c p) s -> p c s", p=128)
        ovs = [o.rearrange("(c p) s -> p c s", p=128) for o in oT_ds]
        xv_in = x_d.rearrange("(g t p) d -> g p t d", p=128, t=2)
        xv_out = x_out.rearrange("(g t p) d -> g p t d", p=128, t=2)

        def load(gi):
            s = gi % 2
            ts = slice(gi * TG, (gi + 1) * TG)
            DMAG(k, "sp", [(hTt[s].ap[:], hv[:, :, ts])] + [(oTt[r][s].ap[:], ovs[r][:, :, ts]) for r in range(3)] + [(xin[s].ap[:], xv_in[gi])],
                 d_i[s], [hTt[s], oTt[0][s], oTt[1][s], oTt[2][s], xin[s]], [])

        load(0)
        it = 0
        for gi in range(NG):
            s = gi % 2
            if gi + 1 < NG:
                load(gi + 1)
            for dc in range(8):
                for r in range(3):
                    i2 = it % 2
                    it += 1
                    MMG(k, pj[i2], [(pj[i2].ap[:, 0:TG], wb.ap[:, r * 8 + kc, dc * 128:(dc + 1) * 128], oTt[r][s].ap[:, kc, :]) for kc in range(8)], [wb, oTt[r][s]])
                    MMG(k, gp[i2], [(gp[i2].ap[:, 0:TG], wg.ap[:, kc, r * D + dc * 128:r * D + (dc + 1) * 128], hTt[s].ap[:, kc, :]) for kc in range(8)], [wg, hTt[s]])
                    gt, tm = gtb[i2], tmpb[i2]
                    OP(k, "act", lambda e, gt=gt, i2=i2, r=r, dc=dc: e.activation(out=gt.ap[:], in_=gp[i2].ap[:, 0:TG], func=AF.Sigmoid, bias=gbias.ap[:, r, dc:dc + 1], scale=1.0), [gt], [gp[i2], gbias])
                    if r == 0:
                        OP(k, "dve", lambda e, gt=gt, i2=i2: e.tensor_tensor(out=macc.ap[:], in0=pj[i2].ap[:, 0:TG], in1=gt.ap[:], op=ALU.mult), [macc], [pj[i2], gt])
                    else:
                        OP(k, "dve", lambda e, gt=gt, tm=tm, i2=i2: e.tensor_tensor(out=tm.ap[:], in0=pj[i2].ap[:, 0:TG], in1=gt.ap[:], op=ALU.mult), [tm], [pj[i2], gt])
                        if r == 1:
                            OP(k, "pool", lambda e, tm=tm: e.tensor_tensor(out=macc.ap[:], in0=macc.ap[:], in1=tm.ap[:], op=ALU.add), [macc], [tm])
                        else:
                            OP(k, "pool", lambda e, tm=tm, dc=dc: e.tensor_tensor(out=mT.ap[:, dc, :], in0=macc.ap[:], in1=tm.ap[:], op=ALU.add), [mT], [macc, tm])
            for t in range(2):
                for n in range(2):
                    MMG(k, py[t], [(py[t].ap[:, n * 512:(n + 1) * 512], mT.ap[:, dc, t * 128:(t + 1) * 128], wo.ap[:, dc, n * 512:(n + 1) * 512]) for dc in range(8)],
                        [mT, wo], first=(n == 0), start=True)
                xt = xin[s].ap[:, t, :]
                X = xin[s]
                OP(k, "act", lambda e, xt=xt: e.mul(out=xt, in_=xt, mul=ALPHA), [X], [])
                OP(k, "dve", lambda e, xt=xt, t=t: e.tensor_tensor(out=xt, in0=py[t].ap, in1=xt, op=ALU.add), [X], [py[t]])
                ln_tail_tb(k, X, xt, gb, bb, st6, mv, sd, rs, nm)
                DMA(k, "sp", xv_out[gi][:, t, :], xt, d_o[s], [], [X])
        k.barrier()


def ln_tail_tb(k, X, xt, gb, bb, st6, mv, sd, rs, nm):
    for h in range(2):
        OP(k, "dve", lambda e, h=h: e.bn_stats(out=st6.ap[:, h, :], in_=xt[:, h * 512:(h + 1) * 512]), [st6], [X])
    OP(k, "dve", lambda e: e.bn_aggr(out=mv.ap[:], in_=st6.ap[:].rearrange("p a b -> p (a b)")), [mv], [st6])
    OP(k, "act", lambda e: e.activation(out=sd.ap[:], in_=mv.ap[:, 1:2], func=AF.Sqrt, bias=LN_EPS_AP[0][:, 0:1], scale=1.0), [sd], [mv], extra=[LN_EPS_AP[1]])
    OP(k, "dve", lambda e: e.reciprocal(out=rs.ap[:], in_=sd.ap[:]), [rs], [sd])
    OP(k, "dve", lambda e: e.tensor_scalar(out=nm.ap[:], in0=mv.ap[:, 0:1], scalar1=rs.ap[:, 0:1], scalar2=-1.0, op0=ALU.mult, op1=ALU.mult), [nm], [mv, rs])
    OP(k, "act", lambda e: e.activation(out=xt, in_=xt, func=AF.Identity, bias=nm.ap[:, 0:1], scale=rs.ap[:, 0:1]), [X], [nm, rs])
    OP(k, "pool", lambda e: e.tensor_tensor(out=xt, in0=xt, in1=gb.ap[:], op=ALU.mult), [X], [gb])
    OP(k, "pool", lambda e: e.tensor_tensor(out=xt, in0=xt, in1=bb.ap[:], op=ALU.add), [X], [bb])


def build_program(S):
    nc = bass.Bass("TRN2", target_bir_lowering=False)
    L = DEPTH
    din = lambda n, sh, t=F32: nc.dram_tensor(n, sh, t, kind="ExternalInput").ap()
    x = din("x", [S, D])
    ffn_w_in = [din("ffn1_w_in", [L, D, 2 * DFF]), din("ffn2_w_in", [L, D, 2 * DFF])]
    ffn_w_out = [din("ffn1_w_out", [L, DFF, D]), din("ffn2_w_out", [L, DFF, D])]
    ln_g = [din("ln%d_g" % i, [L, D]) for i in (1, 2, 3)]
    ln_b = [din("ln%d_b" % i, [L, D]) for i in (1, 2, 3)]
    w_mix = din("w_mix_in", [L, D, NIN])
    gbias = din("gbias_r", [L, 128, 3, 8])
    lam = din("diff_lambda", [L, 256])
    subg = din("subg_r", [L, 128, 1])
    convw = din("convw_r", [L, 128, 12, 4])
    convb = din("convb_r", [L, 128, 12])
    dtb = din("ssm_dt_bias", [L, 16])
    alog = din("ssm_A_log", [L, 16])
    dsk = din("ssm_D", [L, 16])
    ng = din("ssm_norm_g", [L, D])
    w_br = din("w_branch", [L, 3, D, D])
    w_o = din("w_mix_out", [L, D, D])
    cos = din("c_cos", [128, S])
    sin = din("c_sin", [128, S])
    cst = dict(ident=din("c_ident", [128, 128], BF16), maskA=din("c_maskA", [128, 128], BF16), LT=din("c_LT", [128, 128], BF16),
               mS=din("c_mS", [128, 128], BF16), TRI=din("c_TRI", [128, 128]), SGT=din("c_SGT", [128, 128]), MLE=din("c_MLE", [128, 128]))
    out = nc.dram_tensor("out", [S, D], F32, kind="ExternalOutput").ap()
    xflat = x.bitcast(BF16).rearrange("s c -> (s c)").rearrange("(a b) -> a b", b=S)
    hT_s = xflat[0:D, :]
    w0 = ffn_w_in[0][0].bitcast(BF16)
    oT_s = [xflat[D:2 * D, :], w0[:, 0:S], w0[:, S:2 * S]]
    with ExitStack() as es:
        k = K(nc, es)
        setup_consts(k, es)
        cur = x
        for l in range(L):
            emit_ffn(k, cur, out, ffn_w_in[0][l], ffn_w_out[0][l], ln_g[0][l], ln_b[0][l], cst["ident"], S)
            emit_transpose(k, out, hT_s, cst["ident"], S)
            with ExitStack() as esm:
                hT_res = TB(esm.enter_context(nc.sbuf_tensor(uname("hT_res"), [128, 8, S], BF16)))
                emit_diffattn(k, hT_s, oT_s[0], w_mix[l], lam[l], subg[l], cos, sin, cst, S, l, hT_res=hT_res)
                emit_mamba(k, hT_s, oT_s[1], w_mix[l], convw[l], convb[l], dtb[l], alog[l], dsk[l], ng[l], cst, S, hT_res=hT_res)
                emit_stickbreak(k, hT_s, oT_s[2], w_mix[l], cst, S, hT_res=hT_res)
            emit_merge(k, out, hT_s, oT_s, w_mix[l], gbias[l], w_br[l], w_o[l], ln_g[1][l], ln_b[1][l], out, S)
            emit_ffn(k, out, out, ffn_w_in[1][l], ffn_w_out[1][l], ln_g[2][l], ln_b[2][l], cst["ident"], S)
            cur = out
        print("sem counts", {n: e["cnt"] for n, e in k.engs.items()}, "dma", max(d.cnt for d in k.dsems), flush=True)
    return nc


def host_consts(S):
    import ml_dtypes
    bf = ml_dtypes.bfloat16
    half = 8
    inv = (500000.0 ** (-np.arange(half, dtype=np.float32) * 2.0 / 16)).astype(np.float32)
    ang = (np.arange(S, dtype=np.float32)[:, None] * inv[None, :]).astype(np.float32)
    c = np.ones((128, S), np.float32)
    s = np.zeros((128, S), np.float32)
    for m in range(2):
        for d in range(16):
            c[m * 64 + d] = np.cos(ang[:, d % 8])
            s[m * 64 + d] = np.sin(ang[:, d % 8])
    ii = np.arange(128)
    mA = np.ones((128, 128), np.float32)
    mA[64:, :64] = 0
    return dict(c_cos=c, c_sin=s, c_ident=np.eye(128, dtype=np.float32).astype(bf), c_maskA=mA.astype(bf),
                c_LT=(ii[:, None] >= ii[None, :]).astype(np.float32).astype(bf), c_mS=(ii[:, None] < ii[None, :]).astype(np.float32).astype(bf),
                c_TRI=(ii[:, None] <= ii[None, :]).astype(np.float32), c_SGT=(ii[:, None] > ii[None, :]).astype(np.float32),
                c_MLE=(ii[:, None] <= ii[None, :]).astype(np.float32))


def host_layout(inputs):
    f = lambda a: np.ascontiguousarray(np.asarray(a, dtype=np.float32))
    L = DEPTH
    d = {}
    for n in ("ffn1_w_in", "ffn2_w_in", "ffn1_w_out", "ffn2_w_out", "ln1_g", "ln2_g", "ln3_g", "ln1_b", "ln2_b", "ln3_b", "w_mix_in",
              "ssm_dt_bias", "ssm_A_log", "ssm_D", "ssm_norm_g", "w_branch", "w_mix_out"):
        d[n] = f(inputs[n])
    d["diff_lambda"] = f(inputs["diff_lambda"]).reshape(L, 256)
    d["subg_r"] = f(inputs["diff_subln_g"]).reshape(L, 128, 1)
    d["gbias_r"] = f(np.asarray(inputs["gate_bias"]).reshape(L, 3, 8, 128).transpose(0, 3, 1, 2))
    d["convw_r"] = f(np.asarray(inputs["ssm_conv_w"]).reshape(L, 4, 12, 128).transpose(0, 3, 2, 1))
    d["convb_r"] = f(np.asarray(inputs["ssm_conv_b"]).reshape(L, 12, 128).transpose(0, 2, 1))
    return d


_CACHE = {}


def kernel(**inputs):
    x = np.asarray(inputs["x"], dtype=np.float32)
    B, S, _ = x.shape
    if S not in _CACHE:
        _CACHE[S] = build_program(S)
    nc = _CACHE[S]
    shared = host_layout(inputs)
    shared.update(host_consts(S))
    in_maps = []
    for bi in range(B):
        m = dict(shared)
        m["x"] = np.ascontiguousarray(x[bi])
        in_maps.append(m)
    res = run_bass_kernel_spmd(nc, in_maps, core_ids=list(range(B)))
    return np.stack([np.asarray(r["out"]) for r in res.results], axis=0).astype(np.float32)
```
